# Optimizing a Trainium2 kernel written in Bass

```python
import math
import jax, jax.numpy as jnp
from jax import lax
import numpy as np

D_MODEL = 1024
BATCH = 4
SEQ = 4096
DEPTH = 1

N_ATTN_HEADS = 8
HEAD_DIM = 64
ATTN_WIDTH = N_ATTN_HEADS * HEAD_DIM
N_RG_BLOCKS = 8
RG_WIDTH = 512
RG_BLOCK = RG_WIDTH // N_RG_BLOCKS
MIX_WIDTH = ATTN_WIDTH + RG_WIDTH
IN_COLS = 3 * ATTN_WIDTH + 2 * RG_WIDTH
DILATED_PATTERNS = ((128, 1), (512, 4), (2048, 16))
BAND_BLOCK = 64
ROPE_THETA = 10000.0
CONV_WIDTH = 4
CONV_PAD_LEFT = 2
CONV_PAD_RIGHT = 1
RG_C = 8.0
N_MEM = 256
N_CROSS_HEADS = 4
CROSS_HEAD_DIM = 128
CROSS_WIDTH = N_CROSS_HEADS * CROSS_HEAD_DIM
D_FF = 4 * D_MODEL
NORM_EPS = 1e-6
NEG_BIG = -1e30

kernel_name = "hybrid_dilated_attn_rglru_encoder_block"


def rmsnorm(x, g):
    xf = x.astype(jnp.float32)
    return xf * lax.rsqrt(jnp.mean(xf * xf, axis=-1, keepdims=True) + NORM_EPS) * g.astype(jnp.float32)


def rope(t, positions):
    half = t.shape[-1] // 2
    inv_freq = ROPE_THETA ** (-jnp.arange(half, dtype=jnp.float32) / half)
    ang = positions.astype(jnp.float32)[..., None] * inv_freq
    cos = jnp.cos(ang)[:, :, None, :]
    sin = jnp.sin(ang)[:, :, None, :]
    t1, t2 = t[..., :half], t[..., half:]
    return jnp.concatenate([t1 * cos - t2 * sin, t1 * sin + t2 * cos], axis=-1)


def banded_window_attention(q, k, v, radius):
    L, dh = q.shape[-2], q.shape[-1]
    blk = BAND_BLOCK
    nb = -(-L // blk)
    Lp = nb * blk
    lead = q.shape[:-2]
    nlead = len(lead)
    qb = jnp.pad(q, [(0, 0)] * nlead + [(0, Lp - L), (0, 0)]).reshape(*lead, nb, blk, dh)
    pad_both = [(0, 0)] * nlead + [(blk, Lp - L + blk), (0, 0)]
    kb = jnp.pad(k, pad_both).reshape(*lead, nb + 2, blk, dh)
    vb = jnp.pad(v, pad_both).reshape(*lead, nb + 2, blk, dh)
    kn = jnp.concatenate([kb[..., :-2, :, :], kb[..., 1:-1, :, :], kb[..., 2:, :, :]], axis=-2)
    vn = jnp.concatenate([vb[..., :-2, :, :], vb[..., 1:-1, :, :], vb[..., 2:, :, :]], axis=-2)
    s = jnp.einsum('...nqd,...nkd->...nqk', qb, kn) * (dh ** -0.5)
    qpos = jnp.arange(nb)[:, None] * blk + jnp.arange(blk)[None, :]
    kpos = (jnp.arange(nb)[:, None] - 1) * blk + jnp.arange(3 * blk)[None, :]
    dist = qpos[:, :, None] - kpos[:, None, :]
    valid = (jnp.abs(dist) <= radius) & (kpos[:, None, :] >= 0) & (kpos[:, None, :] < L)
    s = jnp.where(valid, s, NEG_BIG)
    m = jnp.max(s, axis=-1, keepdims=True)
    p = jnp.exp(s - m)
    denom = jnp.sum(p, axis=-1, keepdims=True)
    o = jnp.einsum('...nqk,...nkd->...nqd', p, vn) / denom
    lse = (m + jnp.log(denom))[..., 0]
    o = o.reshape(*lead, Lp, dh)[..., :L, :]
    lse = lse.reshape(*lead, Lp)[..., :L]
    return o, lse


def dilated_mixture_attention(q, k, v):
    B, S, H, dh = q.shape
    outs, lses = [], []
    for window, dil in DILATED_PATTERNS:
        L = S // dil

        def to_classes(t):
            return t.reshape(B, L, dil, H, dh).transpose(0, 2, 3, 1, 4)

        o, lse = banded_window_attention(to_classes(q), to_classes(k), to_classes(v), window // (2 * dil))
        outs.append(o.transpose(0, 3, 1, 2, 4).reshape(B, S, H, dh))
        lses.append(lse.transpose(0, 3, 1, 2).reshape(B, S, H))
    w = jax.nn.softmax(jnp.stack(lses, axis=0), axis=0)
    return jnp.sum(w[..., None] * jnp.stack(outs, axis=0), axis=0)


def centred_depthwise_conv(u, w, b):
    out = lax.conv_general_dilated(
        u, w[:, None, :].astype(u.dtype), window_strides=(1,),
        padding=[(CONV_PAD_LEFT, CONV_PAD_RIGHT)],
        dimension_numbers=('NWC', 'WIO', 'NWC'),
        feature_group_count=u.shape[-1])
    return out + b


def _linear_recurrence_combine(c1, c2):
    a1, b1 = c1
    a2, b2 = c2
    return a1 * a2, a2 * b1 + b2


def bidirectional_rglru(u, a_w, a_b, x_w, x_b, lam):
    B, S, _ = u.shape
    ug = u.reshape(B, S, N_RG_BLOCKS, RG_BLOCK)
    r = jax.nn.sigmoid(jnp.einsum('bsgi,zgij->zbsgj', ug, a_w) + a_b[:, None, None]).reshape(2, B, S, RG_WIDTH)
    i = jax.nn.sigmoid(jnp.einsum('bsgi,zgij->zbsgj', ug, x_w) + x_b[:, None, None]).reshape(2, B, S, RG_WIDTH)
    log_a = -RG_C * r * jax.nn.softplus(-lam.astype(jnp.float32))[:, None, None, :]
    a = jnp.exp(log_a)
    b = jnp.sqrt(-jnp.expm1(2.0 * log_a)) * (i * u[None])
    _, h_f = lax.associative_scan(_linear_recurrence_combine, (a[0], b[0]), axis=1)
    _, h_b = lax.associative_scan(_linear_recurrence_combine, (a[1], b[1]), axis=1, reverse=True)
    return h_f + h_b


def memory_cross_attention(xn, memn, wq, wk, wv, wo):
    B, S, _ = xn.shape
    M = memn.shape[1]
    q = (xn @ wq).reshape(B, S, N_CROSS_HEADS, CROSS_HEAD_DIM)
    k = (memn @ wk).reshape(B, M, N_CROSS_HEADS, CROSS_HEAD_DIM)
    v = (memn @ wv).reshape(B, M, N_CROSS_HEADS, CROSS_HEAD_DIM)
    s = jnp.einsum('bshd,bmhd->bhsm', q, k) * (CROSS_HEAD_DIM ** -0.5)
    p = jax.nn.softmax(s, axis=-1)
    o = jnp.einsum('bhsm,bmhd->bshd', p, v).reshape(B, S, CROSS_WIDTH)
    return o @ wo


def setup_inputs(seed: int = 0) -> dict:
    key = jax.random.key(seed)
    ks = jax.random.split(key, 24)
    f32 = jnp.float32

    def nrm(k, shape, scale):
        return jax.random.normal(k, shape, f32) * scale

    def gain(k, shape):
        return 1.0 + 0.05 * jax.random.normal(k, shape, f32)

    lam_u = jax.random.uniform(ks[11], (DEPTH, 2, RG_WIDTH), f32, minval=0.9, maxval=0.999)
    a0 = lam_u ** (1.0 / RG_C)
    rg_lambda = jnp.log(a0) - jnp.log1p(-a0)
    return {
        "x": nrm(ks[0], (BATCH, SEQ, D_MODEL), 1.0),
        "mem": nrm(ks[1], (BATCH, N_MEM, D_MODEL), 1.0),
        "positions": jnp.broadcast_to(jnp.arange(SEQ, dtype=jnp.int32), (BATCH, SEQ)),
        "norm_mix_g": gain(ks[2], (DEPTH, D_MODEL)),
        "w_in": nrm(ks[3], (DEPTH, D_MODEL, IN_COLS), D_MODEL ** -0.5),
        "conv_w": nrm(ks[4], (DEPTH, CONV_WIDTH, RG_WIDTH), CONV_WIDTH ** -0.5),
        "conv_b": nrm(ks[5], (DEPTH, RG_WIDTH), 0.01),
        "rg_a_w": nrm(ks[6], (DEPTH, 2, N_RG_BLOCKS, RG_BLOCK, RG_BLOCK), RG_BLOCK ** -0.5),
        "rg_a_b": nrm(ks[7], (DEPTH, 2, N_RG_BLOCKS, RG_BLOCK), 0.01),
        "rg_x_w": nrm(ks[8], (DEPTH, 2, N_RG_BLOCKS, RG_BLOCK, RG_BLOCK), RG_BLOCK ** -0.5),
        "rg_x_b": nrm(ks[9], (DEPTH, 2, N_RG_BLOCKS, RG_BLOCK), 0.01),
        "rg_lambda": rg_lambda,
        "w_out": nrm(ks[10], (DEPTH, MIX_WIDTH, D_MODEL), MIX_WIDTH ** -0.5),
        "norm_cross_g": gain(ks[12], (DEPTH, D_MODEL)),
        "norm_mem_g": gain(ks[13], (DEPTH, D_MODEL)),
        "w_cq": nrm(ks[14], (DEPTH, D_MODEL, CROSS_WIDTH), D_MODEL ** -0.5),
        "w_ck": nrm(ks[15], (DEPTH, D_MODEL, CROSS_WIDTH), D_MODEL ** -0.5),
        "w_cv": nrm(ks[16], (DEPTH, D_MODEL, CROSS_WIDTH), D_MODEL ** -0.5),
        "w_co": nrm(ks[17], (DEPTH, CROSS_WIDTH, D_MODEL), CROSS_WIDTH ** -0.5),
        "norm_mlp_g": gain(ks[18], (DEPTH, D_MODEL)),
        "w_mlp_in": nrm(ks[19], (DEPTH, D_MODEL, D_FF), D_MODEL ** -0.5),
        "w_mlp_out": nrm(ks[20], (DEPTH, D_FF, D_MODEL), D_FF ** -0.5),
        "norm_final_g": gain(ks[21], (D_MODEL,)),
    }


def reference(x, mem, positions, norm_mix_g, w_in, conv_w, conv_b, rg_a_w, rg_a_b, rg_x_w, rg_x_b,
              rg_lambda, w_out, norm_cross_g, norm_mem_g, w_cq, w_ck, w_cv, w_co,
              norm_mlp_g, w_mlp_in, w_mlp_out, norm_final_g):
    B, S, _ = x.shape
    h = x.astype(jnp.float32)
    memf = mem.astype(jnp.float32)
    for l in range(DEPTH):
        xn = rmsnorm(h, norm_mix_g[l])
        proj = xn @ w_in[l]
        q, k, v, u, y = jnp.split(proj, [ATTN_WIDTH, 2 * ATTN_WIDTH, 3 * ATTN_WIDTH,
                                         3 * ATTN_WIDTH + RG_WIDTH], axis=-1)
        q = rope(q.reshape(B, S, N_ATTN_HEADS, HEAD_DIM), positions)
        k = rope(k.reshape(B, S, N_ATTN_HEADS, HEAD_DIM), positions)
        v = v.reshape(B, S, N_ATTN_HEADS, HEAD_DIM)
        attn = dilated_mixture_attention(q, k, v).reshape(B, S, ATTN_WIDTH)
        uc = centred_depthwise_conv(u, conv_w[l], conv_b[l])
        rec = bidirectional_rglru(uc, rg_a_w[l], rg_a_b[l], rg_x_w[l], rg_x_b[l], rg_lambda[l]) * jax.nn.gelu(y)
        h = h + jnp.concatenate([attn, rec], axis=-1) @ w_out[l]
        h = h + memory_cross_attention(rmsnorm(h, norm_cross_g[l]), rmsnorm(memf, norm_mem_g[l]),
                                       w_cq[l], w_ck[l], w_cv[l], w_co[l])
        hid = jnp.square(jax.nn.relu(rmsnorm(h, norm_mlp_g[l]) @ w_mlp_in[l]))
        h = h + hid @ w_mlp_out[l]
    return rmsnorm(h, norm_final_g).astype(x.dtype)
```

```python
import contextlib
import math
import numpy as np
import concourse.bass as bass
import concourse.mybir as mybir
from concourse.bass_utils import run_bass_kernel_spmd

F32 = mybir.dt.float32
BF16 = mybir.dt.bfloat16
I32 = mybir.dt.int32
AF = mybir.ActivationFunctionType
ALU = mybir.AluOpType

PE, ACT, DVE, POOL, SP = "pe", "act", "dve", "pool", "sp"
ENGS = (PE, ACT, DVE, POOL, SP)

D = 1024
T = 4096
OWN = 2048
KV = 3072
PAD = 1024
NT_OWN = OWN // 128
EPS = 1e-6
PATTERNS = (1, 4, 16)


class Res:
    __slots__ = ("name", "lw", "rd", "dsem", "excl")

    def __init__(self, name):
        self.name = name
        self.lw = None
        self.rd = []
        self.dsem = None
        self.excl = name.startswith(("pb", "ptr", "pvbf"))


class Prog:
    SAME_ENGINE_SYNC = True

    def __init__(self, nc, st, n_eng_sets=13, n_dsem=38):
        self.nc = nc
        self.eng = {PE: nc.tensor, ACT: nc.scalar, DVE: nc.vector, POOL: nc.gpsimd, SP: nc.sync}
        self.semsets = []
        for i in range(n_eng_sets):
            self.semsets.append({e: st.enter_context(nc.semaphore(f"s{i}_{e}")) for e in (PE, ACT, DVE, POOL)})
        self.set_idx = 0
        self.dsems = [st.enter_context(nc.semaphore(f"sd{i}")) for i in range(n_dsem)]
        self.ndsem = 0
        self.count = {}
        self.handle = {}
        for e in (PE, ACT, DVE, POOL):
            k = (e, 0)
            self.handle[k] = self.semsets[0][e]
            self.count[k] = 0
        self.seen = {e: {} for e in ENGS}
        self.ninst = {e: 0 for e in ENGS}

    def ekey(self, e):
        return (e, self.set_idx)

    def _waits(self, eng, reads, writes, skip=None):
        deps = {}

        def add(ev):
            if ev is None:
                return
            k, v = ev
            if v > deps.get(k, 0):
                deps[k] = v
        for r in reads:
            add(r.lw)
            if r.excl:
                for ev in r.rd:
                    if ev[0][0] != eng:
                        add(ev)
        for w in writes:
            add(w.lw)
            for ev in w.rd:
                add(ev)
        for k, v in deps.items():
            if k == skip:
                continue
            if k[0] == eng and (eng == PE or not self.SAME_ENGINE_SYNC):
                continue
            if self.seen[eng].get(k, 0) >= v:
                continue
            assert self.count.get(k, 0) >= v, f"{eng} waits on {k}>={v}, only {self.count.get(k, 0)} emitted"
            self.seen[eng][k] = v
            self.eng[eng].wait_ge(self.handle[k], v)

    dead = False

    def op(self, eng, fn, reads=(), writes=(), inc=True):
        if self.dead:
            return None
        self._waits(eng, reads, writes)
        ins = fn(self.eng[eng])
        self.ninst[eng] += 1
        k = self.ekey(eng)
        if inc:
            self.count[k] += 1
            ins.then_inc(self.handle[k], 1)
            ev = (k, self.count[k])
        else:
            ev = (k, self.count[k] + 1)
        for r in reads:
            r.rd.append(ev)
            if len(r.rd) > 64:
                r.rd = self._compress(r.rd)
        for w in writes:
            w.lw = ev
            w.rd = []
        return ev

    @staticmethod
    def _compress(evs):
        d = {}
        for k, v in evs:
            if v > d.get(k, 0):
                d[k] = v
        return list(d.items())

    def dma(self, queue, fn, reads=(), writes=(), sres=None, par=False):
        if self.dead:
            return None
        if sres is None:
            sres = writes[0] if writes else reads[0]
        if sres.dsem is None:
            assert self.ndsem < len(self.dsems), "out of dma semaphores"
            sres.dsem = ("d", self.ndsem)
            self.handle[sres.dsem] = self.dsems[self.ndsem]
            self.count[sres.dsem] = 0
            self.ndsem += 1
        self._waits(queue, reads, writes, skip=(sres.dsem if par else None))
        ins = fn(self.eng[queue])
        self.ninst[queue] += 1
        k = sres.dsem
        self.count[k] += 16
        ins.then_inc(self.handle[k], 16)
        ev = (k, self.count[k])
        for r in reads:
            r.rd.append(ev)
        for w in writes:
            w.lw = ev
            w.rd = []
        return ev

    def barrier(self):
        if self.dead:
            return
        for e in ENGS:
            for k, v in self.count.items():
                if v == 0 or (k[0] == e):
                    continue
                if self.seen[e].get(k, 0) >= v:
                    continue
                self.seen[e][k] = v
                self.eng[e].wait_ge(self.handle[k], v)
        self.set_idx += 1
        assert self.set_idx < len(self.semsets), "out of engine semaphore sets"
        for e in (PE, ACT, DVE, POOL):
            k = (e, self.set_idx)
            self.handle[k] = self.semsets[self.set_idx][e]
            self.count[k] = 0

    def final_wait(self, evs):
        if self.dead:
            return
        for k, v in evs:
            self.eng[SP].wait_ge(self.handle[k], v)


def run_threads(threads):
    pending = dict(threads)
    active = {}
    done = set()
    while pending or active:
        for n in list(pending):
            fn, deps = pending[n]
            if all(d_ in done for d_ in deps):
                active[n] = fn()
                del pending[n]
        assert active, f"thread deadlock: {list(pending)}"
        for n in list(active):
            try:
                next(active[n])
            except StopIteration:
                done.add(n)
                del active[n]


class _Stop(Exception):
    pass


def build_program(debug=False, stop_after=None):
    nc = bass.Bass("TRN2", target_bir_lowering=False)

    def din(name, shape, dt=F32):
        return nc.dram_tensor(name, list(shape), dt, kind="ExternalInput").ap()

    x_d = din("x", [T, D])
    mem_d = din("mem", [256, D])
    pos_d = din("pos", [1, T], I32)
    gains_d = din("gains", [5, D])
    w_in_d = din("w_in", [D, 2560])
    w_out_d = din("w_out", [D, D])
    w_cq_d = din("w_cq", [D, 512])
    w_ck_d = din("w_ck", [D, 512])
    w_cv_d = din("w_cv", [D, 512])
    w_co_d = din("w_co", [512, D])
    w1_d = din("w_mlp_in", [D, 4096])
    w2_d = din("w_mlp_out", [4096, D])
    chp_d = din("chan_params", [512, 12])
    rga_d = din("rg_a_w", [2, 8, 64, 64])
    rgx_d = din("rg_x_w", [2, 8, 64, 64])
    ident_d = din("c_ident", [128, 128])
    mask_d = din("c_mask", [128, 512])
    ropec_d = din("c_rope", [128, 3])
    perm_d = din("c_perm", [128, 128])
    out_d = nc.dram_tensor("out", [OWN, D], F32, kind="ExternalOutput").ap()
    dbg = {}
    if debug:
        dbg["mixT"] = nc.dram_tensor("dbg_mixT", [128, 8, OWN], BF16, kind="ExternalOutput").ap()
        dbg["h1"] = nc.dram_tensor("dbg_h1", [128, NT_OWN, D], F32, kind="ExternalOutput").ap()
        dbg["h2"] = nc.dram_tensor("dbg_h2", [128, NT_OWN, D], F32, kind="ExternalOutput").ap()
        dbg["xnT"] = nc.dram_tensor("dbg_xnT", [128, 8, T], BF16, kind="ExternalOutput").ap()

    with contextlib.ExitStack() as top:
        P = Prog(nc, top)
        try:
            _build_body(nc, top, P, locals())
        except _Stop:
            P.mixscope.close()
        except BaseException:
            P.mixscope.close()
            raise
        print("instruction counts", P.ninst, "dma sems", P.ndsem, flush=True)
    return nc


def _build_body(nc, top, P, env):
    globals_ = env
    debug = env["debug"]; stop_after = env["stop_after"]; dbg = env["dbg"]
    x_d = env["x_d"]; mem_d = env["mem_d"]; pos_d = env["pos_d"]; gains_d = env["gains_d"]; w_in_d = env["w_in_d"]
    w_out_d = env["w_out_d"]; w_cq_d = env["w_cq_d"]; w_ck_d = env["w_ck_d"]; w_cv_d = env["w_cv_d"]; w_co_d = env["w_co_d"]
    w1_d = env["w1_d"]; w2_d = env["w2_d"]; chp_d = env["chp_d"]; rga_d = env["rga_d"]; rgx_d = env["rgx_d"]
    ident_d = env["ident_d"]; mask_d = env["mask_d"]; ropec_d = env["ropec_d"]; out_d = env["out_d"]; perm_d = env["perm_d"]
    if True:

        def sb(st, name, shape, dt=F32, side=None):
            return st.enter_context(nc.sbuf_tensor(name, list(shape), dt, side=side))

        def ps(st, name, shape, dt=F32):
            return st.enter_context(nc.psum_tensor(name, list(shape), dt))

        ident = sb(top, "ident", [128, 128], BF16)
        gB = sb(top, "gB", [128, D])
        ones_bf = sb(top, "ones_bf", [128, 128], BF16)
        onespad_bf = sb(top, "onespad_bf", [128, 128], BF16)
        mixscope = contextlib.ExitStack()
        P.mixscope = mixscope
        mixT = sb(mixscope, "mixT", [128, 8, OWN], BF16, side="right")
        R_ident, R_gB, R_ones, R_mixT = Res("ident"), Res("gB"), Res("ones"), Res("mixT")
        R_mixk = [Res(f"mixT{k}") for k in range(8)]

        P.dma(POOL, lambda e: e.dma_start(out=ident[:], in_=ident_d), writes=[R_ident])
        P.op(DVE, lambda e: e.memset(ones_bf[:], 1.0), writes=[R_ones])
        P.op(DVE, lambda e: e.memset(onespad_bf[:], 1.0), writes=[R_ones])
        P.op(DVE, lambda e: e.memset(onespad_bf[0:64, :], 0.0), writes=[R_ones])

        ckpt_state = {}

        def checkpoint(name):
            if stop_after != name:
                return
            evs = []
            if name == "A":
                evs.append(P.dma(SP, lambda e: e.dma_start(out=dbg["xnT"], in_=ckpt_state["xnT"][:]), reads=[], sres=Res("dbgx")))
            else:
                evs.append(P.dma(SP, lambda e: e.dma_start(out=dbg["mixT"], in_=mixT[:]), reads=[], sres=Res("dbgm")))
            P.final_wait(evs)
            P.dead = True

        def load_gain(idx):
            P.dma(SP, lambda e: e.dma_start(out=gB[:], in_=gains_d[idx:idx + 1, :].broadcast_to([128, D])), writes=[R_gB])

        def norm_transpose(st, tag, n_tiles, src_ap_fn, src_res_fn, dstT, dst_col0, dst_res_fn, ptr, R_ptr, load_fn=None):
            junk = sb(st, f"junk_{tag}", [128, D], BF16)
            ss = sb(st, f"ss_{tag}", [128, n_tiles])
            rstd = sb(st, f"rstd_{tag}", [128, n_tiles])
            LA = 2
            xnb = [sb(st, f"xnb_{tag}{j}", [128, D], BF16) for j in range(LA + 1)]
            R_junk = Res("junk")
            R_ss = [Res(f"ss{i}") for i in range(n_tiles)]
            R_xnb = [Res(f"xnb{j}") for j in range(LA + 1)]

            def stage1(i):
                if load_fn is not None:
                    load_fn(i)
                src = src_ap_fn(i)
                rs = src_res_fn(i)
                P.op(ACT, lambda e: e.activation(out=junk[:], in_=src, func=AF.Square, accum_out=ss[:, i:i + 1]),
                     reads=[rs], writes=[R_junk, R_ss[i]])
                P.op(ACT, lambda e: e.activation(out=rstd[:, i:i + 1], in_=ss[:, i:i + 1], func=AF.Sqrt, scale=1.0 / D, bias=EPS),
                     reads=[R_ss[i]], writes=[R_ss[i]])
                P.op(DVE, lambda e: e.reciprocal(out=rstd[:, i:i + 1], in_=rstd[:, i:i + 1]), reads=[R_ss[i]], writes=[R_ss[i]])
                xb, rxb = xnb[i % (LA + 1)], R_xnb[i % (LA + 1)]
                P.op(DVE, lambda e: e.scalar_tensor_tensor(out=xb[:], in0=src, scalar=rstd[:, i:i + 1], in1=gB[:],
                                                           op0=ALU.mult, op1=ALU.mult),
                     reads=[rs, R_ss[i], R_gB], writes=[rxb])

            def stage2(i):
                xb, rxb = xnb[i % (LA + 1)], R_xnb[i % (LA + 1)]
                pt, rpt = ptr[i % 2], R_ptr[i % 2]
                for k in range(8):
                    P.op(PE, lambda e: e.transpose(pt[:, k, :], xb[:, k * 128:(k + 1) * 128], ident[:]),
                         reads=[rxb, R_ident], writes=[rpt], inc=(k == 7))
                c0 = dst_col0 + i * 128
                P.op(ACT, lambda e: e.copy(out=dstT[:, :, c0:c0 + 128], in_=pt[:]), reads=[rpt], writes=[dst_res_fn(i)])

            for j0 in range(min(LA, n_tiles)):
                stage1(j0)
            for i in range(n_tiles):
                if i + LA < n_tiles:
                    stage1(i + LA)
                    yield
                stage2(i)
                yield

        def norm_transpose_batched(st, tag, n_tiles, src_ap_fn, src_res_fn, dstT, dst_res_fn, ptr, R_ptr, ss_pre=None):
            junk = sb(st, f"junk_{tag}", [128, D], BF16)
            ss = sb(st, f"ss_{tag}", [128, n_tiles])
            xnb = [sb(st, f"xnb_{tag}{j}", [128, D], BF16) for j in range(2)]
            R_junk, R_ss = Res("junk"), Res("ss")
            R_xnb = [Res("xnb0"), Res("xnb1")]
            if ss_pre is None:
                for i in range(n_tiles):
                    P.op(ACT, lambda e: e.activation(out=junk[:], in_=src_ap_fn(i), func=AF.Square, accum_out=ss[:, i:i + 1]),
                         reads=[src_res_fn(i)], writes=[R_junk, R_ss])
                P.op(ACT, lambda e: e.activation(out=ss[:], in_=ss[:], func=AF.Sqrt, scale=1.0 / D, bias=EPS), reads=[R_ss], writes=[R_ss])
            else:
                P.op(ACT, lambda e: e.activation(out=ss[:], in_=ss_pre[0], func=AF.Sqrt, scale=1.0 / D, bias=EPS),
                     reads=[ss_pre[1]], writes=[R_ss])
            P.op(DVE, lambda e: e.reciprocal(out=ss[:], in_=ss[:]), reads=[R_ss], writes=[R_ss])
            for i in range(n_tiles):
                xb, rxb = xnb[i % 2], R_xnb[i % 2]
                P.op(DVE, lambda e: e.scalar_tensor_tensor(out=xb[:], in0=src_ap_fn(i), scalar=ss[:, i:i + 1], in1=gB[:],
                                                           op0=ALU.mult, op1=ALU.mult),
                     reads=[src_res_fn(i), R_ss, R_gB], writes=[rxb])
                pt, rpt = ptr[i % 2], R_ptr[i % 2]
                for k in range(8):
                    P.op(PE, lambda e: e.transpose(pt[:, k, :], xb[:, k * 128:(k + 1) * 128], ident[:]),
                         reads=[rxb, R_ident], writes=[rpt], inc=(k == 7))
                c0 = i * 128
                P.op(ACT, lambda e: e.copy(out=dstT[:, :, c0:c0 + 128], in_=pt[:]), reads=[rpt], writes=[dst_res_fn(i)])

        with contextlib.ExitStack() as mixst:
            xnT = sb(mixst, "xnT", [128, 8, T], BF16)
            R_xn = [Res(f"xn{i}") for i in range(T // 128)]
            R_xpad = Res("xpad")
            load_gain(0)
            cosT = sb(mixst, "cosT", [128, KV], BF16)
            sinT = sb(mixst, "sinT", [128, KV], BF16)
            ropec = sb(mixst, "ropec", [128, 3])
            R_cos, R_sin, R_ropec = Res("cos"), Res("sin"), Res("ropec")
            P.dma(SP, lambda e: e.dma_start(out=ropec[:], in_=ropec_d), writes=[R_ropec])

            rgscope = contextlib.ExitStack()
            chp = sb(rgscope, "chp", [128, 4, 12], side="right")
            cz = sb(rgscope, "cz", [128, 4, 2], side="right")
            czh = sb(rgscope, "czh", [128, 4, 2], side="right")
            chph = sb(rgscope, "chph", [128, 4, 4], side="right")
            gw = sb(rgscope, "gw", [128, 4, 2, 2, 128], BF16, side="right")
            cbt = sb(rgscope, "cbt", [128, 4], side="right")
            R_chp, R_cz, R_gw = Res("chp"), Res("cz"), Res("gw")
            R_cb = [Res(f"cb{c}") for c in range(4)]
            P.dma(SP, lambda e: e.dma_start(out=chp[:], in_=chp_d.rearrange("(ct p) n -> p ct n", p=128)), writes=[R_chp])
            P.op(POOL, lambda e: e.memset(gw[:], 0.0), writes=[R_gw])
            for ct in range(4):
                for z in range(2):
                    for ai, wd in enumerate((rga_d, rgx_d)):
                        for hb in range(2):
                            P.dma(POOL, lambda e: e.dma_start(out=gw[hb * 64:(hb + 1) * 64, ct, z, ai, hb * 64:(hb + 1) * 64],
                                                              in_=wd[z, 2 * ct + hb, :, :]), writes=[R_gw], par=True)
            P.op(ACT, lambda e: e.activation(out=cz[:], in_=chp[:, :, 10:12], func=AF.Exp, scale=-1.0), reads=[R_chp], writes=[R_cz])
            P.op(ACT, lambda e: e.activation(out=cz[:], in_=cz[:], func=AF.Ln, bias=1.0, scale=1.0), reads=[R_cz], writes=[R_cz])
            P.op(DVE, lambda e: e.tensor_scalar(out=czh[:], in0=cz[:], scalar1=-4.0, scalar2=None, op0=ALU.mult), reads=[R_cz], writes=[R_cz])
            P.op(DVE, lambda e: e.tensor_scalar(out=cz[:], in0=cz[:], scalar1=-8.0, scalar2=None, op0=ALU.mult), reads=[R_cz], writes=[R_cz])
            P.op(DVE, lambda e: e.tensor_scalar(out=chph[:], in0=chp[:, :, 6:10], scalar1=0.5, scalar2=None, op0=ALU.mult), reads=[R_chp], writes=[R_cz])


            with contextlib.ExitStack() as st:
                HK = KV // 2
                posi = sb(st, "posi", [128, KV], I32)
                R_posi = Res("posi")
                P.dma(SP, lambda e: e.dma_start(out=posi[:], in_=pos_d[0:1, 0:KV].broadcast_to([128, KV])), writes=[R_posi])
                ropebufs = []
                for which in range(2):
                    ropebufs.append((sb(st, f"ropey{which}", [128, HK]), sb(st, f"ropeki{which}", [128, HK], I32),
                                     sb(st, f"ropeg{which}", [128, HK]), Res("y"), Res("ki"), Res("g1")))

                def rope_thread():
                    EN = DVE
                    for which, (dst, rdst) in enumerate(((sinT, R_sin), (cosT, R_cos))):
                        y, ki, g1, R_y, R_ki, R_g1 = ropebufs[which]
                        for hh in range(2):
                            cs = slice(hh * HK, (hh + 1) * HK)
                            P.op(EN, lambda e: e.tensor_scalar(out=y[:], in0=posi[:, cs], scalar1=ropec[:, 0:1], scalar2=0.25 * which,
                                                               op0=ALU.mult, op1=ALU.add), reads=[R_posi, R_ropec], writes=[R_y])
                            yield
                            P.op(EN, lambda e: e.scalar_tensor_tensor(out=y[:], in0=posi[:, cs], scalar=ropec[:, 2:3], in1=y[:],
                                                                      op0=ALU.mult, op1=ALU.add), reads=[R_posi, R_ropec, R_y], writes=[R_y])
                            yield
                            P.op(EN, lambda e: e.tensor_copy(out=ki[:], in_=y[:]), reads=[R_y], writes=[R_ki])
                            yield
                            P.op(EN, lambda e: e.tensor_tensor(out=y[:], in0=y[:], in1=ki[:], op=ALU.subtract), reads=[R_y, R_ki], writes=[R_y])
                            yield
                            P.op(EN, lambda e: e.scalar_tensor_tensor(out=g1[:], in0=y[:], scalar=0.5, in1=y[:], op0=ALU.is_gt, op1=ALU.subtract),
                                 reads=[R_y], writes=[R_g1])
                            yield
                            P.op(EN, lambda e: e.scalar_tensor_tensor(out=y[:], in0=g1[:], scalar=0.5, in1=g1[:], op0=ALU.is_gt, op1=ALU.subtract),
                                 reads=[R_g1], writes=[R_y])
                            yield
                            P.op(EN, lambda e: e.tensor_scalar(out=y[:], in0=y[:], scalar1=0.4999999, scalar2=-0.4999999, op0=ALU.min, op1=ALU.max),
                                 reads=[R_y], writes=[R_y])
                            yield
                            P.op(ACT, lambda e: e.activation(out=y[:], in_=y[:], func=AF.Sin, scale=2.0 * math.pi), reads=[R_y], writes=[R_y])
                            yield
                            if which == 0:
                                P.op(EN, lambda e: e.tensor_scalar(out=dst[:, cs], in0=y[:], scalar1=ropec[:, 1:2], scalar2=None, op0=ALU.mult),
                                     reads=[R_y, R_ropec], writes=[rdst])
                            else:
                                P.op(EN, lambda e: e.tensor_copy(out=dst[:, cs], in_=y[:]), reads=[R_y], writes=[rdst])
                            yield

                xs = [sb(st, f"xs{j}", [128, D]) for j in range(4)]
                R_xs = [Res(f"xs{j}") for j in range(4)]
                ptr = [ps(st, f"ptrA{j}", [128, 8, 128], BF16) for j in range(2)]
                R_ptr = [Res("ptr0"), Res("ptr1")]

                def load_x(i):
                    P.dma(SP if i % 2 == 0 else POOL, lambda e: e.dma_start(out=xs[i % 4][:], in_=x_d[i * 128:(i + 1) * 128, :]),
                          writes=[R_xs[i % 4]])
                run_threads({
                    "norm": (lambda: norm_transpose(st, "A", T // 128, lambda i: xs[i % 4][:], lambda i: R_xs[i % 4], xnT, 0,
                                                    lambda i: R_xn[i], ptr, R_ptr, load_fn=load_x), []),
                    "rope": (rope_thread, []),
                })
                P.barrier()
                ckpt_state["xnT"] = xnT
                checkpoint("A")

            def xn_res(tok0, tok1):
                rs = []
                if tok0 < 0:
                    rs.append(R_xpad)
                for i in range(max(tok0, 0) // 128, (max(tok1, 1) + 127) // 128):
                    rs.append(R_xn[i])
                return rs

            with contextlib.ExitStack() as st:
                HB = 2048
                U = [sb(st, f"U{j}", [128, HB + 4]) for j in range(2)]
                UC = [sb(st, f"UC{j}", [128, HB]) for j in range(2)]
                UCb = [sb(st, f"UCb{j}", [128, HB], BF16) for j in range(2)]
                R_UCb = [Res("UCb0"), Res("UCb1")]
                A = [sb(st, f"A{j}", [128, HB]) for j in range(2)]
                Bv = [sb(st, f"Bv{j}", [128, HB]) for j in range(2)]
                Hk = sb(st, "Hk", [128, HB])
                GY = sb(st, "GY", [128, OWN], BF16)
                wu = [sb(st, f"wu{j}", [128, 8, 128], BF16) for j in range(2)]
                wy = [sb(st, "wy0", [128, 8, 128], BF16)] * 2
                R_U = [Res("U0"), Res("U1")]
                R_UC = [Res("UC0"), Res("UC1")]
                R_A = [Res("A0"), Res("A1")]
                R_Bv = [Res("Bv0"), Res("Bv1")]
                R_Hk, R_GY = Res("Hk"), Res("GY")
                R_wu = [Res("wu0"), Res("wu1")]
                R_wy = [Res("wy0")] * 2
                pb = [ps(st, f"pbB{j}", [128, 512]) for j in range(7)]
                R_pb = [Res(f"pbB{j}") for j in range(7)]
                pbi = [0]

                def next_pb():
                    j = pbi[0] % 7
                    pbi[0] += 1
                    return pb[j], R_pb[j]

                def load_w_rg(ct):
                    c0 = 1536 + ct * 128
                    P.dma(POOL, lambda e: e.dma_start(out=wu[ct % 2][:], in_=w_in_d[:, c0:c0 + 128].rearrange("(k p) c -> p k c", p=128)),
                          writes=[R_wu[ct % 2]])
                    c1 = 2048 + ct * 128
                    P.dma(POOL, lambda e: e.dma_start(out=wy[ct % 2][:], in_=w_in_d[:, c1:c1 + 128].rearrange("(k p) c -> p k c", p=128)),
                          writes=[R_wy[ct % 2]])

                ritems = [(ct, hf) for ct in range(4) for hf in (1, 0)]

                def rg_P(i):
                    ct, hf = ritems[i]
                    s_ = i % 2
                    t0 = HB * hf
                    if hf == 1:
                        load_w_rg(ct)
                    wuc, rwu, wyc, rwy = wu[ct % 2], R_wu[ct % 2], wy[ct % 2], R_wy[ct % 2]
                    Us, rU = U[s_], R_U[s_]
                    for tb in range(HB // 512):
                        pt, rpt = next_pb()
                        c0 = t0 + tb * 512
                        for k in range(8):
                            P.op(PE, lambda e: e.matmul(pt[:], lhsT=wuc[:, k, :], rhs=xnT[:, k, c0:c0 + 512],
                                                        start=(k == 0), stop=(k == 7)),
                                 reads=[rwu] + xn_res(c0, c0 + 512), writes=[rpt], inc=(k == 7))
                        P.op(ACT, lambda e: e.copy(out=Us[:, 2 + tb * 512:2 + (tb + 1) * 512], in_=pt[:]), reads=[rpt], writes=[rU])
                        yield
                    pt, rpt = next_pb()
                    hb0 = HB - 128 if hf == 1 else HB
                    for k in range(8):
                        P.op(PE, lambda e: e.matmul(pt[:, 0:128], lhsT=wuc[:, k, :], rhs=xnT[:, k, hb0:hb0 + 128],
                                                    start=(k == 0), stop=(k == 7)),
                             reads=[rwu] + xn_res(hb0, hb0 + 128), writes=[rpt], inc=(k == 7))
                    if hf == 1:
                        P.op(ACT, lambda e: e.copy(out=Us[:, 0:2], in_=pt[:, 126:128]), reads=[rpt], writes=[rU])
                        P.op(POOL, lambda e: e.memset(Us[:, HB + 2:HB + 4], 0.0), writes=[rU])
                    else:
                        P.op(ACT, lambda e: e.copy(out=Us[:, HB + 2:HB + 4], in_=pt[:, 0:2]), reads=[rpt], writes=[rU])
                        P.op(POOL, lambda e: e.memset(Us[:, 0:2], 0.0), writes=[rU])
                        for tb in range(OWN // 512):
                            pt, rpt = next_pb()
                            for k in range(8):
                                P.op(PE, lambda e: e.matmul(pt[:], lhsT=wyc[:, k, :], rhs=xnT[:, k, tb * 512:(tb + 1) * 512],
                                                            start=(k == 0), stop=(k == 7)),
                                     reads=[rwy] + xn_res(tb * 512, (tb + 1) * 512), writes=[rpt], inc=(k == 7))
                            gsl = GY[:, tb * 512:(tb + 1) * 512]
                            P.op(ACT, lambda e: e.activation(out=gsl, in_=pt[:], func=AF.Identity, scale=0.5), reads=[rpt], writes=[R_GY])
                            P.op(ACT, lambda e: e.activation(out=Hk[:, tb * 512:(tb + 1) * 512], in_=pt[:], func=AF.Square, scale=0.5),
                                 reads=[rpt, R_Hk], writes=[R_Hk])
                            yield
                        P.op(DVE, lambda e: e.tensor_scalar(out=Hk[:], in0=Hk[:], scalar1=4.0 * 0.044715, scalar2=1.0, op0=ALU.mult, op1=ALU.add),
                             reads=[R_Hk], writes=[R_Hk])
                        yield
                        P.op(DVE, lambda e: e.tensor_tensor(out=Hk[:], in0=Hk[:], in1=GY[:], op=ALU.mult), reads=[R_Hk, R_GY], writes=[R_Hk])
                        yield
                        P.op(ACT, lambda e: e.activation(out=Hk[:], in_=Hk[:], func=AF.Tanh, scale=2.0 * math.sqrt(2.0 / math.pi)), reads=[R_Hk], writes=[R_Hk])
                        yield
                        P.op(DVE, lambda e: e.scalar_tensor_tensor(out=GY[:], in0=Hk[:], scalar=1.0, in1=GY[:], op0=ALU.add, op1=ALU.mult),
                             reads=[R_Hk, R_GY], writes=[R_GY])
                        yield
                    yield

                def rg_C(i):
                    ct, hf = ritems[i]
                    s_ = i % 2
                    Us, rU, UCs, rUC = U[s_], R_U[s_], UC[s_], R_UC[s_]
                    P.op(DVE, lambda e: e.tensor_scalar(out=UCs[:], in0=Us[:, 0:HB], scalar1=chp[:, ct, 0:1], scalar2=chp[:, ct, 5:6],
                                                        op0=ALU.mult, op1=ALU.add), reads=[rU, R_chp], writes=[rUC])
                    yield
                    for j in range(1, 5):
                        P.op(DVE, lambda e: e.scalar_tensor_tensor(out=UCs[:], in0=Us[:, j:j + HB], scalar=chp[:, ct, j:j + 1], in1=UCs[:],
                                                                   op0=ALU.mult, op1=ALU.add), reads=[rU, R_chp, rUC], writes=[rUC])
                        yield

                def rg_G(i, z):
                    ct, hf = ritems[i]
                    s_ = i % 2
                    P.op(ACT, lambda e: e.copy(out=UCb[s_][:], in_=UC[s_][:]), reads=[R_UC[s_]], writes=[R_UCb[s_]])
                    yield
                    for tb in range(HB // 512):
                        sl = slice(tb * 512, (tb + 1) * 512)
                        pr, rpr = next_pb()
                        P.op(PE, lambda e: e.matmul(pr[:], lhsT=gw[:, ct, z, 0, :], rhs=UCb[s_][:, sl], start=True, stop=True),
                             reads=[R_gw, R_UCb[s_]], writes=[rpr])
                        P.op(ACT, lambda e: e.activation(out=A[s_][:, sl], in_=pr[:], func=AF.Tanh, bias=chph[:, ct, z:z + 1], scale=0.5),
                             reads=[rpr, R_cz], writes=[R_A[s_]])
                        pi_, rpi = next_pb()
                        P.op(PE, lambda e: e.matmul(pi_[:], lhsT=gw[:, ct, z, 1, :], rhs=UCb[s_][:, sl], start=True, stop=True),
                             reads=[R_gw, R_UCb[s_]], writes=[rpi])
                        P.op(ACT, lambda e: e.activation(out=Bv[s_][:, sl], in_=pi_[:], func=AF.Tanh, bias=chph[:, ct, 2 + z:3 + z], scale=0.5),
                             reads=[rpi, R_cz], writes=[R_Bv[s_]])
                        yield

                def rg_E(i, z):
                    ct, hf = ritems[i]
                    s_ = i % 2
                    As, rA, Bs, rB, Us, rU, UCs, rUC = A[s_], R_A[s_], Bv[s_], R_Bv[s_], U[s_], R_U[s_], UC[s_], R_UC[s_]
                    Sx = Us[:, 0:HB]
                    P.op(ACT, lambda e: e.activation(out=Sx, in_=As[:], func=AF.Exp, scale=cz[:, ct, z:z + 1], bias=cz[:, ct, z:z + 1]),
                         reads=[rA, R_cz, rU], writes=[rU])
                    yield
                    P.op(ACT, lambda e: e.activation(out=As[:], in_=As[:], func=AF.Exp, scale=czh[:, ct, z:z + 1], bias=czh[:, ct, z:z + 1]),
                         reads=[rA, R_cz], writes=[rA])
                    yield
                    P.op(DVE, lambda e: e.scalar_tensor_tensor(out=Bs[:], in0=Bs[:], scalar=1.0, in1=UCs[:], op0=ALU.add, op1=ALU.mult),
                         reads=[rB, rUC], writes=[rB])
                    yield
                    P.op(ACT, lambda e: e.activation(out=Sx, in_=Sx, func=AF.Sqrt, scale=-0.25, bias=0.25), reads=[rU], writes=[rU])
                    yield
                    P.op(DVE, lambda e: e.tensor_tensor(out=Bs[:], in0=Bs[:], in1=Sx, op=ALU.mult), reads=[rB, rU], writes=[rB])
                    yield
                    if z == 1 and hf == 1:
                        P.op(DVE, lambda e: e.tensor_tensor_scan(out=Sx[:, ::-1], data0=As[:, ::-1], data1=Bs[:, ::-1], initial=0.0,
                                                                 op0=ALU.mult, op1=ALU.add), reads=[rA, rB, rU], writes=[rU])
                        yield
                        P.op(DVE, lambda e: e.tensor_copy(out=cbt[:, ct:ct + 1], in_=Us[:, 0:1]), reads=[rU], writes=[R_cb[ct]])
                    elif z == 1:
                        P.op(DVE, lambda e: e.tensor_tensor_scan(out=Hk[:, ::-1], data0=As[:, ::-1], data1=Bs[:, ::-1], initial=cbt[:, ct:ct + 1],
                                                                 op0=ALU.mult, op1=ALU.add), reads=[rA, rB, R_cb[ct], R_Hk], writes=[R_Hk])
                    else:
                        P.op(DVE, lambda e: e.tensor_tensor_scan(out=Sx, data0=As[:], data1=Bs[:], initial=0.0,
                                                                 op0=ALU.mult, op1=ALU.add), reads=[rA, rB, rU], writes=[rU])
                        yield
                        P.op(DVE, lambda e: e.tensor_tensor(out=Sx, in0=Sx, in1=Hk[:], op=ALU.add), reads=[rU, R_Hk], writes=[rU])
                        yield
                        P.op(DVE, lambda e: e.tensor_tensor(out=mixT[:, 4 + ct, :], in0=Sx, in1=GY[:], op=ALU.mult),
                             reads=[rU, R_GY], writes=[R_mixk[4 + ct]])

                threads = {}
                nI = len(ritems)
                lastG = [None]

                def gdeps(base):
                    return list(base)

                for i in range(nI):
                    lo = ritems[i][1] == 0
                    last_prev2 = None
                    if i >= 2:
                        last_prev2 = f"E{i - 2}_0" if ritems[i - 2][1] == 0 else f"E{i - 2}_1"
                    threads[f"P{i}"] = ((lambda i=i: rg_P(i)), ([last_prev2] if last_prev2 else []) + ([f"P{i - 1}"] if i > 0 else []))
                    threads[f"C{i}"] = ((lambda i=i: rg_C(i)), [f"P{i}"])
                    threads[f"G{i}_1"] = ((lambda i=i: rg_G(i, 1)), gdeps([f"C{i}"]))
                    lastG[0] = f"G{i}_1"
                    threads[f"E{i}_1"] = ((lambda i=i: rg_E(i, 1)), [f"G{i}_1"] + ([f"E{i - 1}_1"] if lo else []))
                    if lo:
                        threads[f"G{i}_0"] = ((lambda i=i: rg_G(i, 0)), gdeps([f"E{i}_1"]))
                        lastG[0] = f"G{i}_0"
                        threads[f"E{i}_0"] = ((lambda i=i: rg_E(i, 0)), [f"G{i}_0"])
                run_threads(threads)
                P.barrier()
                checkpoint("B")
            rgscope.close()

            with contextlib.ExitStack() as st:
                wo = sb(mixscope, "wo", [128, 8, D], BF16, side="right")
                R_wo = Res("wo")
                P.wo = (wo, R_wo)
                maskLU = sb(st, "maskLU", [128, 512], BF16)
                permT = sb(st, "permT", [128, 128], BF16)
                qb = [sb(st, f"qb{j}", [128, 512], BF16) for j in range(2)]
                R_mask, R_perm = Res("mask"), Res("perm")
                R_qb = [Res("qb0"), Res("qb1")]
                P.dma(POOL, lambda e: e.dma_start(out=maskLU[:], in_=mask_d), writes=[R_mask])
                P.dma(POOL, lambda e: e.dma_start(out=permT[:], in_=perm_d), writes=[R_perm])
                KrT = sb(st, "KrT", [128, PAD + KV], BF16)
                QX = [sb(st, f"QX{X}", [128, OWN], BF16) for X in range(2)]
                Vd = sb(st, "Vd", [128, 32, 128], BF16)
                accOD = sb(st, "accOD", [128, 2, OWN])
                accO = accOD[:, 0, :]
                accD = accOD[:, 1, :]
                t1 = [sb(st, f"t1_{j}", [128, 512]) for j in range(2)]
                t2 = [sb(st, f"t2_{j}", [128, 512]) for j in range(2)]
                Et = [[sb(st, f"E{X}{j}", [128, 512], BF16) for j in range(2)] for X in range(2)]
                wset = [{n: sb(st, f"{n}{j}", [128, 8, 128], BF16) for n in ("wq", "wk", "wv")} for j in range(2)]
                R_wset = [{n: Res(f"{n}{j}") for n in ("wq", "wk", "wv")} for j in range(2)]
                R_Kr, R_Qr, R_Vd, R_accO, R_accD = (Res(n) for n in ("Kr", "Qr", "Vd", "accO", "accD"))
                R_t1 = [Res("t10"), Res("t11")]
                R_t2 = [Res("t20"), Res("t21")]
                R_E = [[Res(f"E{X}{j}") for j in range(2)] for X in range(2)]
                pb = [ps(st, f"pbC{j}", [128, 512]) for j in range(7)]
                R_pb = [Res(f"pbC{j}") for j in range(7)]
                pvbf = ps(st, "pvbf", [128, 8, 128], BF16)
                R_pvbf = Res("pvbf")
                P.op(POOL, lambda e: e.memset(KrT[:, 0:PAD], 0.0), writes=[R_Kr])
                P.op(POOL, lambda e: e.memset(QX[0][64:128, :], 0.0), writes=[R_Qr])
                P.op(POOL, lambda e: e.memset(QX[1][0:64, :], 0.0), writes=[R_Qr])

                def wsrc(c0, n=128):
                    return w_in_d[:, c0:c0 + n].rearrange("(k p) c -> p k c", p=128)

                def load_pair_weights(hp):
                    cq, ck, cv = hp * 128, 512 + hp * 128, 1024 + hp * 128
                    for n, c0 in (("wk", ck), ("wq", cq), ("wv", cv)):
                        P.dma(POOL, lambda e: e.dma_start(out=wset[hp % 2][n][:], in_=wsrc(c0)), writes=[R_wset[hp % 2][n]])

                pbi = [0]

                def next_pb(lo=0, hi=3):
                    j = lo + pbi[0] % (hi - lo)
                    pbi[0] += 1
                    return pb[j], R_pb[j]

                VT = sb(st, "VT", [128, PAD + KV], BF16)
                R_VT = Res("VT")
                P.op(POOL, lambda e: e.memset(VT[:, 0:PAD], 0.0), writes=[R_VT])
                E4 = [Et[0][0], Et[0][1], Et[1][0], Et[1][1]]
                R_E4 = [R_E[0][0], R_E[0][1], R_E[1][0], R_E[1][1]]
                LOOK = 2
                sidx = [0]
                odx = [0, 0]

                load_pair_weights(0)
                P.dma(POOL, lambda e: e.dma_start(out=wo[:], in_=w_out_d.rearrange("(k p) c -> p k c", p=128)), writes=[R_wo])
                for hp in range(4):
                    if hp + 1 < 4:
                        load_pair_weights(hp + 1)
                    wq, wk, wv = (wset[hp % 2][n] for n in ("wq", "wk", "wv"))
                    R_w = R_wset[hp % 2]
                    pj = [0]

                    def rope_s1(tb, wa, rwa):
                        xr = xn_res(tb * 512, (tb + 1) * 512)
                        pa, rpa = next_pb(0, 7)
                        for k in range(8):
                            P.op(PE, lambda e: e.matmul(pa[:], lhsT=wa[:, k, :], rhs=xnT[:, k, tb * 512:(tb + 1) * 512],
                                                        start=(k == 0), stop=(k == 7)), reads=[rwa] + xr, writes=[rpa], inc=(k == 7))
                        u = pj[0] % 2
                        pj[0] += 1
                        P.op(ACT, lambda e: e.copy(out=qb[u][:], in_=pa[:]), reads=[rpa], writes=[R_qb[u]])
                        return (pa, rpa, u)

                    def rope_s2(tb, isq, pa, rpa, u):
                        sl = slice(tb * 512, (tb + 1) * 512)
                        pbb, rpb_ = next_pb(0, 7)
                        P.op(PE, lambda e: e.matmul(pbb[:], lhsT=permT[:], rhs=qb[u][:], start=True, stop=True),
                             reads=[R_perm, R_qb[u]], writes=[rpb_])
                        P.op(DVE, lambda e: e.tensor_tensor(out=t1[u][:], in0=pa[:], in1=cosT[:, sl], op=ALU.mult),
                             reads=[rpa, R_cos], writes=[R_t1[u]])
                        P.op(DVE, lambda e: e.tensor_tensor(out=t2[u][:], in0=pbb[:], in1=sinT[:, sl], op=ALU.mult),
                             reads=[rpb_, R_sin], writes=[R_t2[u]])
                        if not isq:
                            P.op(POOL, lambda e: e.tensor_tensor(out=KrT[:, PAD + tb * 512:PAD + (tb + 1) * 512], in0=t1[u][:], in1=t2[u][:], op=ALU.add),
                                 reads=[R_t1[u], R_t2[u]], writes=[R_Kr])
                        else:
                            for X in range(2):
                                hsx = slice(64 * X, 64 * X + 64)
                                P.op(POOL, lambda e: e.tensor_tensor(out=QX[X][hsx, sl], in0=t1[u][hsx, :], in1=t2[u][hsx, :], op=ALU.add),
                                     reads=[R_t1[u], R_t2[u]], writes=[R_Qr])

                    rjobs = []
                    for tb in range(KV // 512):
                        rjobs.append((tb, False))
                        if tb < OWN // 512:
                            rjobs.append((tb, True))
                    prev = None
                    for ji, (tb, isq) in enumerate(rjobs):
                        cur = (tb, isq) + rope_s1(tb, wq if isq else wk, R_w["wq"] if isq else R_w["wk"])
                        if prev is not None:
                            rope_s2(*prev)
                        prev = cur
                        if not isq:
                            xr = xn_res(tb * 512, (tb + 1) * 512)
                            pv, rpv = next_pb(0, 7)
                            for k in range(8):
                                P.op(PE, lambda e: e.matmul(pv[:], lhsT=wv[:, k, :], rhs=xnT[:, k, tb * 512:(tb + 1) * 512],
                                                            start=(k == 0), stop=(k == 7)), reads=[R_w["wv"]] + xr, writes=[rpv], inc=(k == 7))
                            P.op(ACT, lambda e: e.copy(out=VT[:, PAD + tb * 512:PAD + (tb + 1) * 512], in_=pv[:]), reads=[rpv], writes=[R_VT])
                    rope_s2(*prev)

                    def key_tok0(d, r, i):
                        return d * (128 * i - 64) + r

                    def build_V(d):
                        nb = 16 // d
                        ntile = d * (nb + 1)
                        for t0 in range(0, ntile, 8):
                            rpv = R_pvbf
                            pvb = pvbf[:].rearrange("p a b -> p (a b)")
                            n = min(8, ntile - t0)
                            for c in range(n):
                                r, i = divmod(t0 + c, nb + 1)
                                a0 = PAD + key_tok0(d, r, i)
                                P.op(PE, lambda e: e.transpose(pvb[:, c * 128:(c + 1) * 128], VT[:, a0:a0 + 127 * d + 1:d], ident[:]),
                                     reads=[R_VT, R_ident], writes=[rpv], inc=(c == n - 1))
                            P.op(ACT, lambda e: e.copy(out=Vd[:, t0:t0 + n, :], in_=pvb[:, 0:n * 128].rearrange("p (a b) -> p a b", b=128)),
                                 reads=[rpv], writes=[R_Vd])

                    def emit_S(it):
                        pi_, d, nb, u, X, qbs = it
                        hs = slice(64 * X, 64 * X + 64)
                        j = sidx[0] % 4
                        js = sidx[0] % 3
                        sidx[0] += 1
                        psb, rps = pb[js], R_pb[js]
                        Ex, rEx = E4[j], R_E4[j]
                        P.op(PE, lambda e: e.matmul(psb[:], lhsT=ident[:], rhs=maskLU[:], start=True, stop=True),
                             reads=[R_ident, R_mask], writes=[rps], inc=False)
                        shared = (qbs[0][0] == qbs[1][0] and qbs[1][1] == qbs[0][1] + 1)
                        if shared:
                            r, m = qbs[0]
                            q0 = d * 128 * m + r
                            plan = [(m, 0, 0, 128), (m + 1, 0, 128, 256), (m + 2, 128, 384, 128)]
                            for pi2, (kt, qoff, c0, n) in enumerate(plan):
                                a0 = PAD + key_tok0(d, r, kt)
                                qs = q0 + d * qoff
                                P.op(PE, lambda e: e.matmul(psb[:, c0:c0 + n], lhsT=KrT[:, a0:a0 + 127 * d + 1:d],
                                                            rhs=QX[X][:, qs:qs + (n - 1) * d + 1:d], start=False, stop=True,
                                                            skip_group_check=True),
                                     reads=[R_Kr, R_Qr], writes=[rps], inc=(pi2 == 2))
                        else:
                            for uu in range(2):
                                r, m = qbs[uu]
                                q0 = d * 128 * m + r
                                for ti in range(2):
                                    a0 = PAD + key_tok0(d, r, m + ti)
                                    c0 = uu * 256 + ti * 128
                                    P.op(PE, lambda e: e.matmul(psb[:, c0:c0 + 128], lhsT=KrT[:, a0:a0 + 127 * d + 1:d],
                                                                rhs=QX[X][:, q0:q0 + 127 * d + 1:d], start=False, stop=True,
                                                                skip_group_check=True),
                                         reads=[R_Kr, R_Qr], writes=[rps], inc=(uu == 1 and ti == 1))
                        P.op(ACT, lambda e: e.activation(out=Ex[:], in_=psb[:], func=AF.Exp, scale=0.125), reads=[rps], writes=[rEx])
                        return (Ex, rEx)

                    def emit_PV(it, Ex, rEx):
                        pi_, d, nb, u, X, qbs = it
                        hs = slice(64 * X, 64 * X + 64)
                        jb = 3 + 2 * X + (odx[X] % 2)
                        odx[X] += 1
                        pOD, rOD = pb[jb], R_pb[jb]
                        shared = (qbs[0][0] == qbs[1][0] and qbs[1][1] == qbs[0][1] + 1)
                        if shared:
                            r, m = qbs[0]
                            vt0 = r * (nb + 1) + m
                            plan = [(vt0 + 1, 128, 256, 0, True), (vt0, 0, 128, 0, False), (vt0 + 2, 384, 128, 128, False)]
                            for (vt, c0, n, oc, first) in plan:
                                P.op(PE, lambda e: e.matmul(pOD[:, oc:oc + n], lhsT=Vd[:, vt, :], rhs=Ex[:, c0:c0 + n],
                                                            start=first, stop=True, skip_group_check=True),
                                     reads=[R_Vd, rEx], writes=[rOD], inc=False)
                            for pi2, (vt, c0, n, oc, first) in enumerate(plan):
                                on = onespad_bf if (vt == vt0 and m == 0) else ones_bf
                                P.op(PE, lambda e: e.matmul(pOD[:, 256 + oc:256 + oc + n], lhsT=on[:], rhs=Ex[:, c0:c0 + n],
                                                            start=first, stop=True, skip_group_check=True),
                                     reads=[R_ones, rEx], writes=[rOD], inc=(pi2 == 2))
                        else:
                            for uu in range(2):
                                r, m = qbs[uu]
                                for ti in range(2):
                                    c0 = uu * 256 + ti * 128
                                    vt = r * (nb + 1) + m + ti
                                    P.op(PE, lambda e: e.matmul(pOD[:, uu * 128:(uu + 1) * 128], lhsT=Vd[:, vt, :], rhs=Ex[:, c0:c0 + 128],
                                                                start=(ti == 0), stop=(ti == 1)),
                                         reads=[R_Vd, rEx], writes=[rOD], inc=False)
                                for ti in range(2):
                                    c0 = uu * 256 + ti * 128
                                    on = onespad_bf if (m + ti) == 0 else ones_bf
                                    P.op(PE, lambda e: e.matmul(pOD[:, 256 + uu * 128:256 + (uu + 1) * 128], lhsT=on[:], rhs=Ex[:, c0:c0 + 128],
                                                                start=(ti == 0), stop=(ti == 1)),
                                         reads=[R_ones, rEx], writes=[rOD], inc=(uu == 1 and ti == 1))
                        r, m = qbs[0]
                        if d == 1:
                            dst = accOD[hs, :, m * 128:(m + 2) * 128]
                            src = pOD[hs, :].rearrange("p (t c) -> p t c", t=2)
                        elif d == 4:
                            b0 = r + 512 * m
                            dst = accOD[hs, :, b0:b0 + 255 * 4 + 1:4]
                            src = pOD[hs, :].rearrange("p (t c) -> p t c", t=2)
                        else:
                            dst = accOD[:].rearrange("p t (j r) -> p t r j", r=16)[hs, :, r:r + 2, :]
                            src = pOD[hs, :].rearrange("p (t r j) -> p t r j", t=2, j=128)
                        if pi_ == 0:
                            P.op(DVE, lambda e: e.tensor_copy(out=dst, in_=src), reads=[rOD], writes=[R_accO, R_accD])
                        else:
                            P.op(DVE, lambda e: e.tensor_tensor(out=dst, in0=dst, in1=src, op=ALU.add), reads=[rOD, R_accO, R_accD],
                                 writes=[R_accO, R_accD])

                    items = []
                    for pi_, d in enumerate(PATTERNS):
                        nb = 16 // d
                        qblocks = [(r, m) for r in range(d) for m in range(nb)]
                        for u in range(8):
                            for X in range(2):
                                items.append((pi_, d, nb, u, X, (qblocks[2 * u], qblocks[2 * u + 1])))
                    pending = []
                    cur = None
                    for it in items:
                        if it[0] != cur:
                            while pending:
                                emit_PV(*pending.pop(0))
                            build_V(it[1])
                            cur = it[0]
                        ex = emit_S(it)
                        pending.append((it,) + ex)
                        if len(pending) > LOOK:
                            emit_PV(*pending.pop(0))
                    while pending:
                        emit_PV(*pending.pop(0))
                    P.op(ACT, lambda e: e.activation(out=accD, in_=accD, func=AF.Ln), reads=[R_accD], writes=[R_accD])
                    P.op(ACT, lambda e: e.activation(out=accD, in_=accD, func=AF.Exp, scale=-1.0), reads=[R_accD], writes=[R_accD])
                    P.op(DVE, lambda e: e.tensor_tensor(out=mixT[:, hp, :], in0=accO, in1=accD, op=ALU.mult),
                         reads=[R_accO, R_accD], writes=[R_mixk[hp]])
                P.barrier()
                checkpoint("C")

        if debug:
            P.dma(SP, lambda e: e.dma_start(out=dbg["mixT"], in_=mixT[:]), reads=R_mixk, sres=R_mixT)
            P.barrier()

        h = sb(top, "h", [128, NT_OWN, D])
        R_h = [Res(f"h{i}") for i in range(NT_OWN)]
        ssN = sb(top, "ssN", [128, 3, NT_OWN])
        junkN = sb(top, "junkN", [128, D], BF16)
        R_ssN = [Res(f"ssN{j}") for j in range(3)]
        R_junkN = Res("junkN")

        wcq = sb(top, "wcq", [128, 8, 512], BF16)
        wck = sb(top, "wck", [128, 8, 512], BF16)
        wcv = sb(top, "wcv", [128, 8, 512], BF16)
        wco = sb(top, "wco", [128, 4, D], BF16)
        ms = [sb(top, f"ms{j}", [128, D]) for j in range(2)]
        R_wc = {n: Res(n) for n in ("wcq", "wck", "wcv", "wco")}
        R_ms = [Res("ms0"), Res("ms1")]
        for (wt, wd, n) in ((wck, w_ck_d, "wck"), (wcv, w_cv_d, "wcv"), (wcq, w_cq_d, "wcq"), (wco, w_co_d, "wco")):
            P.dma(POOL, lambda e: e.dma_start(out=wt[:], in_=wd.rearrange("(k p) c -> p k c", p=128)), writes=[R_wc[n]])
        for j in range(2):
            P.dma(SP, lambda e: e.dma_start(out=ms[j][:], in_=mem_d[j * 128:(j + 1) * 128, :]), writes=[R_ms[j]])

        def emit_sumsq(j, i):
            P.op(ACT, lambda e: e.activation(out=junkN[:], in_=h[:, i, :], func=AF.Square, accum_out=ssN[:, j, i:i + 1]),
                 reads=[R_h[i]], writes=[R_junkN, R_ssN[j]])

        def accumulate_proj(st, tag, lhsT_fn, lhs_res, nk, w_sb, R_wsb, first_x=False, banks=None, ss_next=None):
            if banks is None:
                pbs = [ps(st, f"pb{tag}{j}", [128, 512]) for j in range(4)]
                R_pbs = [Res(f"pb{tag}{j}") for j in range(4)]
            else:
                pbs, R_pbs = banks
            xs = R_xs = None
            if first_x:
                xs = [sb(st, f"xs{tag}{j}", [128, D]) for j in range(2)]
                R_xs = [Res("xsa"), Res("xsb")]
            c = 0
            for i in range(NT_OWN):
                if first_x:
                    P.dma(SP, lambda e: e.dma_start(out=xs[i % 2][:], in_=x_d[i * 128:(i + 1) * 128, :]), writes=[R_xs[i % 2]])
                for nh in range(2):
                    pt, rpt = pbs[c % 4], R_pbs[c % 4]
                    c += 1
                    for k in range(nk):
                        P.op(PE, lambda e: e.matmul(pt[:], lhsT=lhsT_fn(k, i), rhs=w_sb[:, k, nh * 512:(nh + 1) * 512],
                                                    start=(k == 0), stop=(k == nk - 1)),
                             reads=[lhs_res(k), R_wsb], writes=[rpt], inc=(k == nk - 1))
                    hsl = h[:, i, nh * 512:(nh + 1) * 512]
                    if first_x:
                        P.op(DVE, lambda e: e.tensor_tensor(out=hsl, in0=pt[:], in1=xs[i % 2][:, nh * 512:(nh + 1) * 512], op=ALU.add),
                             reads=[rpt, R_xs[i % 2]], writes=[R_h[i]])
                    else:
                        P.op(DVE, lambda e: e.tensor_tensor(out=hsl, in0=pt[:], in1=hsl, op=ALU.add), reads=[rpt, R_h[i]], writes=[R_h[i]])
                if ss_next is not None:
                    emit_sumsq(ss_next, i)

        with contextlib.ExitStack() as st:
            wo, R_wo = P.wo
            accumulate_proj(st, "D", lambda k, i: mixT[:, k, i * 128:(i + 1) * 128], lambda k: R_mixk[k], 8, wo, R_wo, first_x=True, ss_next=0)
            P.barrier()
        mixscope.close()
        if debug:
            P.dma(SP, lambda e: e.dma_start(out=dbg["h1"], in_=h[:]), reads=R_h, sres=R_h[0])
            P.barrier()

        with contextlib.ExitStack() as st:
            hnT = sb(st, "hnT", [128, 8, OWN], BF16)
            R_hn = [Res(f"hn{i}") for i in range(NT_OWN)]
            with contextlib.ExitStack() as st1:
                memnT = sb(st1, "memnT", [128, 8, 256], BF16)
                KcT = sb(st1, "KcT", [128, 4, 256], BF16)
                Vc = sb(st1, "Vc", [128, 2, 512], BF16)
                OcT = sb(st1, "OcT", [128, 4, OWN], BF16)
                R_memn, R_Kc, R_Vc = Res("memn"), Res("Kc"), Res("Vc")
                R_Oc = [Res(f"Oc{k}") for k in range(4)]
                with contextlib.ExitStack() as st2:
                    ptr = [ps(st2, f"ptrE{j}", [128, 8, 128], BF16) for j in range(2)]
                    R_ptr = [Res("ptrE0"), Res("ptrE1")]
                    load_gain(2)
                    for _ in norm_transpose(st2, "Em", 2, lambda i: ms[i][:], lambda i: R_ms[i], memnT, 0, lambda i: R_memn, ptr, R_ptr):
                        pass
                    load_gain(1)
                    norm_transpose_batched(st2, "Eh", NT_OWN, lambda i: h[:, i, :], lambda i: R_h[i], hnT, lambda i: R_hn[i], ptr, R_ptr,
                                           ss_pre=(ssN[:, 0, :], R_ssN[0]))
                    P.barrier()
                pb = [ps(st1, f"pbE{j}", [128, 512]) for j in range(8)]
                R_pb = [Res(f"pbE{j}") for j in range(8)]
                qt = [sb(st1, f"qt{j}", [128, 512], BF16) for j in range(2)]
                R_qt = [Res("qt0"), Res("qt1")]
                Ec = [[sb(st1, f"Ec{j}{m}", [128, 512], BF16) for m in range(2)] for j in range(2)]
                R_Ec = [[Res(f"Ec{j}{m}") for m in range(2)] for j in range(2)]
                rD = [sb(st1, f"rDc{j}", [128, 512]) for j in range(2)]
                R_rD = [Res("rD0"), Res("rD1")]
                for hc in range(4):
                    pt, rpt = pb[hc % 2], R_pb[hc % 2]
                    for k in range(8):
                        P.op(PE, lambda e: e.matmul(pt[:, 0:256], lhsT=wck[:, k, hc * 128:(hc + 1) * 128], rhs=memnT[:, k, :],
                                                    start=(k == 0), stop=(k == 7)), reads=[R_wc["wck"], R_memn], writes=[rpt], inc=(k == 7))
                    P.op(ACT, lambda e: e.copy(out=KcT[:, hc, :], in_=pt[:, 0:256]), reads=[rpt], writes=[R_Kc])
                for mt in range(2):
                    pt, rpt = pb[2 + mt], R_pb[2 + mt]
                    for k in range(8):
                        P.op(PE, lambda e: e.matmul(pt[:], lhsT=memnT[:, k, mt * 128:(mt + 1) * 128], rhs=wcv[:, k, :],
                                                    start=(k == 0), stop=(k == 7)), reads=[R_wc["wcv"], R_memn], writes=[rpt], inc=(k == 7))
                    P.op(ACT, lambda e: e.copy(out=Vc[:, mt, :], in_=pt[:]), reads=[rpt], writes=[R_Vc])
                def cross_a(it, tb, hc):
                    sl = slice(tb * 512, (tb + 1) * 512)
                    hr = [R_hn[i] for i in range(tb * 4, tb * 4 + 4)]
                    j = it % 2
                    pq, rpq = pb[0 + j], R_pb[0 + j]
                    for k in range(8):
                        P.op(PE, lambda e: e.matmul(pq[:], lhsT=wcq[:, k, hc * 128:(hc + 1) * 128], rhs=hnT[:, k, sl],
                                                    start=(k == 0), stop=(k == 7)), reads=[R_wc["wcq"]] + hr, writes=[rpq], inc=(k == 7))
                    P.op(DVE, lambda e: e.tensor_copy(out=qt[j][:], in_=pq[:]), reads=[rpq], writes=[R_qt[j]])
                    for mt in range(2):
                        pS, rS = pb[2 + 2 * j + mt], R_pb[2 + 2 * j + mt]
                        P.op(PE, lambda e: e.matmul(pS[:], lhsT=KcT[:, hc, mt * 128:(mt + 1) * 128], rhs=qt[j][:], start=True, stop=True),
                             reads=[R_Kc, R_qt[j]], writes=[rS])
                        P.op(ACT, lambda e: e.activation(out=Ec[j][mt][:], in_=pS[:], func=AF.Exp, scale=128.0 ** -0.5),
                             reads=[rS], writes=[R_Ec[j][mt]])

                def cross_b(it, tb, hc):
                    sl = slice(tb * 512, (tb + 1) * 512)
                    j = it % 2
                    pO, rO = pb[6], R_pb[6]
                    pD, rDp = pb[7], R_pb[7]
                    for mt in range(2):
                        P.op(PE, lambda e: e.matmul(pO[:], lhsT=Vc[:, mt, hc * 128:(hc + 1) * 128], rhs=Ec[j][mt][:],
                                                    start=(mt == 0), stop=(mt == 1)), reads=[R_Vc, R_Ec[j][mt]], writes=[rO], inc=(mt == 1))
                    for mt in range(2):
                        P.op(PE, lambda e: e.matmul(pD[:], lhsT=ones_bf[:], rhs=Ec[j][mt][:],
                                                    start=(mt == 0), stop=(mt == 1)), reads=[R_ones, R_Ec[j][mt]], writes=[rDp], inc=(mt == 1))
                    P.op(ACT, lambda e: e.activation(out=rD[j][:], in_=pD[:], func=AF.Ln), reads=[rDp], writes=[R_rD[j]])
                    P.op(ACT, lambda e: e.activation(out=rD[j][:], in_=rD[j][:], func=AF.Exp, scale=-1.0), reads=[R_rD[j]], writes=[R_rD[j]])
                    P.op(DVE, lambda e: e.tensor_tensor(out=OcT[:, hc, sl], in0=pO[:], in1=rD[j][:], op=ALU.mult),
                         reads=[rO, R_rD[j]], writes=[R_Oc[hc]])

                citems = [(tb, hc) for tb in range(OWN // 512) for hc in range(4)]
                cross_a(0, *citems[0])
                for it in range(len(citems)):
                    if it + 1 < len(citems):
                        cross_a(it + 1, *citems[it + 1])
                    cross_b(it, *citems[it])
                P.barrier()
                with contextlib.ExitStack() as st3:
                    accumulate_proj(st3, "E", lambda k, i: OcT[:, k, i * 128:(i + 1) * 128], lambda k: R_Oc[k], 4, wco, R_wc["wco"], banks=(pb[0:4], R_pb[0:4]), ss_next=1)
                    P.barrier()
            if debug:
                P.dma(SP, lambda e: e.dma_start(out=dbg["h2"], in_=h[:]), reads=R_h, sres=R_h[0])
                P.barrier()

            with contextlib.ExitStack() as st1:
                NG = 8
                w1g = [sb(st1, f"w1g{j}", [128, 8, 512], BF16) for j in range(2)]
                w2g = [sb(st1, f"w2g{j}", [128, 4, D], BF16) for j in range(2)]
                R_w1g = [Res("w1g0"), Res("w1g1")]
                R_w2g = [Res("w2g0"), Res("w2g1")]

                def load_mlp_w(g):
                    j = g % 2
                    P.dma(POOL, lambda e: e.dma_start(out=w1g[j][:], in_=w1_d[:, g * 512:(g + 1) * 512].rearrange("(k p) c -> p k c", p=128)),
                          writes=[R_w1g[j]])
                    P.dma(POOL, lambda e: e.dma_start(out=w2g[j][:], in_=w2_d[g * 512:(g + 1) * 512, :].rearrange("(k p) c -> p k c", p=128)),
                          writes=[R_w2g[j]])

                load_mlp_w(0)
                with contextlib.ExitStack() as st2:
                    ptr = [ps(st2, f"ptrF{j}", [128, 8, 128], BF16) for j in range(2)]
                    R_ptr = [Res("ptrF0"), Res("ptrF1")]
                    load_gain(3)
                    norm_transpose_batched(st2, "F", NT_OWN, lambda i: h[:, i, :], lambda i: R_h[i], hnT, lambda i: R_hn[i], ptr, R_ptr,
                                           ss_pre=(ssN[:, 1, :], R_ssN[1]))
                    P.barrier()
                hid = sb(st1, "hid", [128, 4, OWN], BF16)
                R_hid = [[Res(f"hid{ft}_{tb}") for tb in range(4)] for ft in range(4)]
                rl = [sb(st1, f"rl{j}", [128, 512], BF16) for j in range(2)]
                R_rl = [Res("rl0"), Res("rl1")]
                pb = [ps(st1, f"pbF{j}", [128, 512]) for j in range(8)]
                R_pb = [Res(f"pbF{j}") for j in range(8)]

                c1 = 0
                c2 = 0
                for g in range(NG):
                    if g + 1 < NG:
                        load_mlp_w(g + 1)
                    j = g % 2
                    for tb in range(4):
                        sl = slice(tb * 512, (tb + 1) * 512)
                        hr = [R_hn[i] for i in range(tb * 4, tb * 4 + 4)]
                        for ft in range(4):
                            pt, rpt = pb[c1 % 4], R_pb[c1 % 4]
                            u = c1 % 2
                            c1 += 1
                            for k in range(8):
                                P.op(PE, lambda e: e.matmul(pt[:], lhsT=w1g[j][:, k, ft * 128:(ft + 1) * 128], rhs=hnT[:, k, sl],
                                                            start=(k == 0), stop=(k == 7)), reads=[R_w1g[j]] + hr, writes=[rpt], inc=(k == 7))
                            P.op(ACT, lambda e: e.activation(out=rl[u][:], in_=pt[:], func=AF.Relu), reads=[rpt], writes=[R_rl[u]])
                            P.op(POOL, lambda e: e.tensor_tensor(out=hid[:, ft, sl], in0=rl[u][:], in1=rl[u][:], op=ALU.mult),
                                 reads=[R_rl[u]], writes=[R_hid[ft][tb]])
                    for i in range(NT_OWN):
                        for nh in range(2):
                            pt, rpt = pb[4 + c2 % 4], R_pb[4 + c2 % 4]
                            c2 += 1
                            for ft in range(4):
                                P.op(PE, lambda e: e.matmul(pt[:], lhsT=hid[:, ft, i * 128:(i + 1) * 128], rhs=w2g[j][:, ft, nh * 512:(nh + 1) * 512],
                                                            start=(ft == 0), stop=(ft == 3)),
                                     reads=[R_hid[ft][i // 4], R_w2g[j]], writes=[rpt], inc=(ft == 3))
                            hsl = h[:, i, nh * 512:(nh + 1) * 512]
                            P.op(DVE, lambda e: e.tensor_tensor(out=hsl, in0=pt[:], in1=hsl, op=ALU.add), reads=[rpt, R_h[i]], writes=[R_h[i]])
                        if g == NG - 1:
                            emit_sumsq(2, i)
                P.barrier()

        with contextlib.ExitStack() as st:
            junk = sb(st, "junkG", [128, D], BF16)
            ss = sb(st, "ssG", [128, NT_OWN])
            ob = [sb(st, f"ob{j}", [128, D]) for j in range(2)]
            R_junk, R_ss = Res("junkG"), Res("ssG")
            R_ob = [Res("ob0"), Res("ob1")]
            load_gain(4)
            finals = []
            P.op(ACT, lambda e: e.activation(out=ss[:], in_=ssN[:, 2, :], func=AF.Sqrt, scale=1.0 / D, bias=EPS), reads=[R_ssN[2]], writes=[R_ss])
            P.op(DVE, lambda e: e.reciprocal(out=ss[:], in_=ss[:]), reads=[R_ss], writes=[R_ss])
            for i in range(NT_OWN):
                P.op(DVE, lambda e: e.scalar_tensor_tensor(out=ob[i % 2][:], in0=h[:, i, :], scalar=ss[:, i:i + 1], in1=gB[:],
                                                           op0=ALU.mult, op1=ALU.mult), reads=[R_h[i], R_ss, R_gB], writes=[R_ob[i % 2]])
                ev = P.dma(SP, lambda e: e.dma_start(out=out_d[i * 128:(i + 1) * 128, :], in_=ob[i % 2][:]), reads=[R_ob[i % 2]])
                finals.append(ev)
            P.final_wait(P._compress([f for f in finals if f is not None]))


def _consts():
    ident = np.eye(128, dtype=np.float32)
    i = np.arange(128)[:, None]
    j = np.arange(128)[None, :]
    L = np.where(j <= i, 0.0, -30000.0).astype(np.float32)
    U = np.where(j >= i, 0.0, -30000.0).astype(np.float32)
    mask = np.concatenate([L, U, L, U], axis=1)
    p = np.arange(128)
    c64 = (10000.0 ** (-np.arange(32, dtype=np.float64) / 32.0)) / (2.0 * np.pi)
    c_hi = c64.astype(np.float32)
    c_lo = (c64 - c_hi.astype(np.float64)).astype(np.float32)
    ropec = np.zeros((128, 3), np.float32)
    ropec[:, 0] = c_hi[p % 32]
    ropec[:, 1] = np.where((p % 64) < 32, -1.0, 1.0)
    ropec[:, 2] = c_lo[p % 32]
    m = np.arange(128)
    sw = (m // 64) * 64 + ((m % 64) + 32) % 64
    perm = np.zeros((128, 128), np.float32)
    perm[sw, m] = 1.0
    return ident, mask, ropec, perm


def make_in_maps(inputs):
    f = lambda a: np.ascontiguousarray(np.asarray(a))
    x = f(inputs["x"]); mem = f(inputs["mem"]); pos = f(inputs["positions"])
    ident, mask, ropec, perm = _consts()
    gains = np.stack([f(inputs["norm_mix_g"])[0], f(inputs["norm_cross_g"])[0], f(inputs["norm_mem_g"])[0],
                      f(inputs["norm_mlp_g"])[0], f(inputs["norm_final_g"])], axis=0).astype(np.float32)
    conv_w = f(inputs["conv_w"])[0]
    conv_b = f(inputs["conv_b"])[0]
    a_b = f(inputs["rg_a_b"])[0].reshape(2, 512)
    x_b = f(inputs["rg_x_b"])[0].reshape(2, 512)
    lam = f(inputs["rg_lambda"])[0]
    a_w = f(inputs["rg_a_w"])[0]
    x_w = f(inputs["rg_x_w"])[0]
    zero = np.zeros((1, 512), np.float32)
    shared = {
        "gains": gains, "w_in": f(inputs["w_in"])[0], "w_out": f(inputs["w_out"])[0],
        "w_cq": f(inputs["w_cq"])[0], "w_ck": f(inputs["w_ck"])[0], "w_cv": f(inputs["w_cv"])[0], "w_co": f(inputs["w_co"])[0],
        "w_mlp_in": f(inputs["w_mlp_in"])[0], "w_mlp_out": f(inputs["w_mlp_out"])[0],
        "c_ident": ident, "c_mask": mask, "c_rope": ropec, "c_perm": perm,
    }
    in_maps = []
    for b in range(4):
        for rev in range(2):
            m = dict(shared)
            if rev == 0:
                m["x"] = x[b]
                m["pos"] = pos[b][None, :].astype(np.int32)
                conv5 = np.concatenate([conv_w, zero], axis=0)
                z = [0, 1]
            else:
                m["x"] = np.ascontiguousarray(x[b][::-1])
                m["pos"] = np.ascontiguousarray(pos[b][::-1])[None, :].astype(np.int32)
                conv5 = np.concatenate([zero, conv_w[::-1]], axis=0)
                z = [1, 0]
            m["mem"] = mem[b]
            cp = np.concatenate([conv5.T, conv_b[:, None], a_b[z].T, x_b[z].T, lam[z].T], axis=1)
            m["chan_params"] = np.ascontiguousarray(cp.astype(np.float32))
            m["rg_a_w"] = np.ascontiguousarray(a_w[z])
            m["rg_x_w"] = np.ascontiguousarray(x_w[z])
            in_maps.append(m)
    return in_maps


def kernel(**inputs):
    in_maps = make_in_maps(inputs)
    nc = build_program()
    res = run_bass_kernel_spmd(nc, in_maps, core_ids=list(range(8)))
    out = np.empty((4, T, D), np.float32)
    ci = 0
    for b in range(4):
        for rev in range(2):
            o = np.asarray(res.results[ci]["out"])
            if rev == 0:
                out[b, 0:OWN] = o
            else:
                out[b, OWN:T] = o[::-1]
            ci += 1
    return out
```

```python
import contextlib
import math
import numpy as np
import concourse.bass as bass
import concourse.mybir as mybir
from concourse.bass_utils import run_bass_kernel_spmd

F32 = mybir.dt.float32
BF16 = mybir.dt.bfloat16
I32 = mybir.dt.int32
AF = mybir.ActivationFunctionType
ALU = mybir.AluOpType

PE, ACT, DVE, POOL, SP = "pe", "act", "dve", "pool", "sp"
ENGS = (PE, ACT, DVE, POOL, SP)

D = 1024
T = 4096
OWN = 2048
KV = 3072
PAD = 1024
NT_OWN = OWN // 128
EPS = 1e-6
PATTERNS = (1, 4, 16)


class Res:
    __slots__ = ("name", "lw", "rd", "dsem", "excl")

    def __init__(self, name):
        self.name = name
        self.lw = None
        self.rd = []
        self.dsem = None
        self.excl = name.startswith(("pb", "ptr", "pvbf"))


class Prog:
    SAME_ENGINE_SYNC = True

    def __init__(self, nc, st, n_eng_sets=13, n_dsem=38):
        self.nc = nc
        self.eng = {PE: nc.tensor, ACT: nc.scalar, DVE: nc.vector, POOL: nc.gpsimd, SP: nc.sync}
        self.semsets = []
        for i in range(n_eng_sets):
            self.semsets.append({e: st.enter_context(nc.semaphore(f"s{i}_{e}")) for e in (PE, ACT, DVE, POOL)})
        self.set_idx = 0
        self.dsems = [st.enter_context(nc.semaphore(f"sd{i}")) for i in range(n_dsem)]
        self.ndsem = 0
        self.count = {}
        self.handle = {}
        for e in (PE, ACT, DVE, POOL):
            k = (e, 0)
            self.handle[k] = self.semsets[0][e]
            self.count[k] = 0
        self.seen = {e: {} for e in ENGS}
        self.ninst = {e: 0 for e in ENGS}

    def ekey(self, e):
        return (e, self.set_idx)

    def _waits(self, eng, reads, writes, skip=None):
        deps = {}

        def add(ev):
            if ev is None:
                return
            k, v = ev
            if v > deps.get(k, 0):
                deps[k] = v
        for r in reads:
            add(r.lw)
            if r.excl:
                for ev in r.rd:
                    if ev[0][0] != eng:
                        add(ev)
        for w in writes:
            add(w.lw)
            for ev in w.rd:
                add(ev)
        for k, v in deps.items():
            if k == skip:
                continue
            if k[0] == eng and (eng == PE or not self.SAME_ENGINE_SYNC):
                continue
            if self.seen[eng].get(k, 0) >= v:
                continue
            assert self.count.get(k, 0) >= v, f"{eng} waits on {k}>={v}, only {self.count.get(k, 0)} emitted"
            self.seen[eng][k] = v
            self.eng[eng].wait_ge(self.handle[k], v)

    dead = False

    def op(self, eng, fn, reads=(), writes=(), inc=True):
        if self.dead:
            return None
        self._waits(eng, reads, writes)
        ins = fn(self.eng[eng])
        self.ninst[eng] += 1
        k = self.ekey(eng)
        if inc:
            self.count[k] += 1
            ins.then_inc(self.handle[k], 1)
            ev = (k, self.count[k])
        else:
            ev = (k, self.count[k] + 1)
        for r in reads:
            r.rd.append(ev)
            if len(r.rd) > 64:
                r.rd = self._compress(r.rd)
        for w in writes:
            w.lw = ev
            w.rd = []
        return ev

    @staticmethod
    def _compress(evs):
        d = {}
        for k, v in evs:
            if v > d.get(k, 0):
                d[k] = v
        return list(d.items())

    def dma(self, queue, fn, reads=(), writes=(), sres=None, par=False):
        if self.dead:
            return None
        if sres is None:
            sres = writes[0] if writes else reads[0]
        if sres.dsem is None:
            assert self.ndsem < len(self.dsems), "out of dma semaphores"
            sres.dsem = ("d", self.ndsem)
            self.handle[sres.dsem] = self.dsems[self.ndsem]
            self.count[sres.dsem] = 0
            self.ndsem += 1
        self._waits(queue, reads, writes, skip=(sres.dsem if par else None))
        ins = fn(self.eng[queue])
        self.ninst[queue] += 1
        k = sres.dsem
        self.count[k] += 16
        ins.then_inc(self.handle[k], 16)
        ev = (k, self.count[k])
        for r in reads:
            r.rd.append(ev)
        for w in writes:
            w.lw = ev
            w.rd = []
        return ev

    def barrier(self):
        if self.dead:
            return
        for e in ENGS:
            for k, v in self.count.items():
                if v == 0 or (k[0] == e):
                    continue
                if self.seen[e].get(k, 0) >= v:
                    continue
                self.seen[e][k] = v
                self.eng[e].wait_ge(self.handle[k], v)
        self.set_idx += 1
        assert self.set_idx < len(self.semsets), "out of engine semaphore sets"
        for e in (PE, ACT, DVE, POOL):
            k = (e, self.set_idx)
            self.handle[k] = self.semsets[self.set_idx][e]
            self.count[k] = 0

    def final_wait(self, evs):
        if self.dead:
            return
        for k, v in evs:
            self.eng[SP].wait_ge(self.handle[k], v)


def run_threads(threads):
    pending = dict(threads)
    active = {}
    done = set()
    while pending or active:
        for n in list(pending):
            fn, deps = pending[n]
            if all(d_ in done for d_ in deps):
                active[n] = fn()
                del pending[n]
        assert active, f"thread deadlock: {list(pending)}"
        for n in list(active):
            try:
                next(active[n])
            except StopIteration:
                done.add(n)
                del active[n]


class _Stop(Exception):
    pass


def build_program(debug=False, stop_after=None):
    nc = bass.Bass("TRN2", target_bir_lowering=False)

    def din(name, shape, dt=F32):
        return nc.dram_tensor(name, list(shape), dt, kind="ExternalInput").ap()

    x_d = din("x", [T, D])
    mem_d = din("mem", [256, D])
    pos_d = din("pos", [1, T], I32)
    gains_d = din("gains", [5, D])
    w_in_d = din("w_in", [D, 2560])
    w_out_d = din("w_out", [D, D])
    w_cq_d = din("w_cq", [D, 512])
    w_ck_d = din("w_ck", [D, 512])
    w_cv_d = din("w_cv", [D, 512])
    w_co_d = din("w_co", [512, D])
    w1_d = din("w_mlp_in", [D, 4096])
    w2_d = din("w_mlp_out", [4096, D])
    chp_d = din("chan_params", [512, 12])
    rga_d = din("rg_a_w", [2, 8, 64, 64])
    rgx_d = din("rg_x_w", [2, 8, 64, 64])
    ident_d = din("c_ident", [128, 128])
    mask_d = din("c_mask", [128, 512])
    ropec_d = din("c_rope", [128, 3])
    perm_d = din("c_perm", [128, 128])
    out_d = nc.dram_tensor("out", [OWN, D], F32, kind="ExternalOutput").ap()
    dbg = {}
    if debug:
        dbg["mixT"] = nc.dram_tensor("dbg_mixT", [128, 8, OWN], BF16, kind="ExternalOutput").ap()
        dbg["h1"] = nc.dram_tensor("dbg_h1", [128, NT_OWN, D], F32, kind="ExternalOutput").ap()
        dbg["h2"] = nc.dram_tensor("dbg_h2", [128, NT_OWN, D], F32, kind="ExternalOutput").ap()
        dbg["xnT"] = nc.dram_tensor("dbg_xnT", [128, 8, T], BF16, kind="ExternalOutput").ap()

    with contextlib.ExitStack() as top:
        P = Prog(nc, top)
        try:
            _build_body(nc, top, P, locals())
        except _Stop:
            P.mixscope.close()
        except BaseException:
            P.mixscope.close()
            raise
        print("instruction counts", P.ninst, "dma sems", P.ndsem, flush=True)
    return nc


def _build_body(nc, top, P, env):
    globals_ = env
    debug = env["debug"]; stop_after = env["stop_after"]; dbg = env["dbg"]
    x_d = env["x_d"]; mem_d = env["mem_d"]; pos_d = env["pos_d"]; gains_d = env["gains_d"]; w_in_d = env["w_in_d"]
    w_out_d = env["w_out_d"]; w_cq_d = env["w_cq_d"]; w_ck_d = env["w_ck_d"]; w_cv_d = env["w_cv_d"]; w_co_d = env["w_co_d"]
    w1_d = env["w1_d"]; w2_d = env["w2_d"]; chp_d = env["chp_d"]; rga_d = env["rga_d"]; rgx_d = env["rgx_d"]
    ident_d = env["ident_d"]; mask_d = env["mask_d"]; ropec_d = env["ropec_d"]; out_d = env["out_d"]; perm_d = env["perm_d"]
    if True:

        def sb(st, name, shape, dt=F32, side=None):
            return st.enter_context(nc.sbuf_tensor(name, list(shape), dt, side=side))

        def ps(st, name, shape, dt=F32):
            return st.enter_context(nc.psum_tensor(name, list(shape), dt))

        ident = sb(top, "ident", [128, 128], BF16)
        gB = sb(top, "gB", [128, D])
        ones_bf = sb(top, "ones_bf", [128, 128], BF16)
        onespad_bf = sb(top, "onespad_bf", [128, 128], BF16)
        mixscope = contextlib.ExitStack()
        P.mixscope = mixscope
        mixT = sb(mixscope, "mixT", [128, 8, OWN], BF16, side="right")
        R_ident, R_gB, R_ones, R_mixT = Res("ident"), Res("gB"), Res("ones"), Res("mixT")
        R_mixk = [Res(f"mixT{k}") for k in range(8)]

        P.dma(POOL, lambda e: e.dma_start(out=ident[:], in_=ident_d), writes=[R_ident])
        P.op(DVE, lambda e: e.memset(ones_bf[:], 1.0), writes=[R_ones])
        P.op(DVE, lambda e: e.memset(onespad_bf[:], 1.0), writes=[R_ones])
        P.op(DVE, lambda e: e.memset(onespad_bf[0:64, :], 0.0), writes=[R_ones])

        ckpt_state = {}

        def checkpoint(name):
            if stop_after != name:
                return
            evs = []
            if name == "A":
                evs.append(P.dma(SP, lambda e: e.dma_start(out=dbg["xnT"], in_=ckpt_state["xnT"][:]), reads=[], sres=Res("dbgx")))
            else:
                evs.append(P.dma(SP, lambda e: e.dma_start(out=dbg["mixT"], in_=mixT[:]), reads=[], sres=Res("dbgm")))
            P.final_wait(evs)
            P.dead = True

        def load_gain(idx):
            P.dma(SP, lambda e: e.dma_start(out=gB[:], in_=gains_d[idx:idx + 1, :].broadcast_to([128, D])), writes=[R_gB])

        def norm_transpose(st, tag, n_tiles, src_ap_fn, src_res_fn, dstT, dst_col0, dst_res_fn, ptr, R_ptr, load_fn=None):
            junk = sb(st, f"junk_{tag}", [128, D], BF16)
            ss = sb(st, f"ss_{tag}", [128, n_tiles])
            rstd = sb(st, f"rstd_{tag}", [128, n_tiles])
            xnb = [sb(st, f"xnb_{tag}{j}", [128, D], BF16) for j in range(2)]
            R_junk = Res("junk")
            R_ss = [Res(f"ss{i}") for i in range(n_tiles)]
            R_xnb = [Res("xnb0"), Res("xnb1")]

            def stage1(i):
                if load_fn is not None:
                    load_fn(i)
                src = src_ap_fn(i)
                rs = src_res_fn(i)
                P.op(ACT, lambda e: e.activation(out=junk[:], in_=src, func=AF.Square, accum_out=ss[:, i:i + 1]),
                     reads=[rs], writes=[R_junk, R_ss[i]])
                P.op(ACT, lambda e: e.activation(out=rstd[:, i:i + 1], in_=ss[:, i:i + 1], func=AF.Sqrt, scale=1.0 / D, bias=EPS),
                     reads=[R_ss[i]], writes=[R_ss[i]])
                P.op(DVE, lambda e: e.reciprocal(out=rstd[:, i:i + 1], in_=rstd[:, i:i + 1]), reads=[R_ss[i]], writes=[R_ss[i]])
                xb, rxb = xnb[i % 2], R_xnb[i % 2]
                P.op(DVE, lambda e: e.scalar_tensor_tensor(out=xb[:], in0=src, scalar=rstd[:, i:i + 1], in1=gB[:],
                                                           op0=ALU.mult, op1=ALU.mult),
                     reads=[rs, R_ss[i], R_gB], writes=[rxb])

            def stage2(i):
                xb, rxb = xnb[i % 2], R_xnb[i % 2]
                pt, rpt = ptr[i % 2], R_ptr[i % 2]
                for k in range(8):
                    P.op(PE, lambda e: e.transpose(pt[:, k, :], xb[:, k * 128:(k + 1) * 128], ident[:]),
                         reads=[rxb, R_ident], writes=[rpt], inc=(k == 7))
                c0 = dst_col0 + i * 128
                P.op(ACT, lambda e: e.copy(out=dstT[:, :, c0:c0 + 128], in_=pt[:]), reads=[rpt], writes=[dst_res_fn(i)])

            stage1(0)
            for i in range(n_tiles):
                if i + 1 < n_tiles:
                    stage1(i + 1)
                    yield
                stage2(i)
                yield

        def norm_transpose_batched(st, tag, n_tiles, src_ap_fn, src_res_fn, dstT, dst_res_fn, ptr, R_ptr, ss_pre=None):
            junk = sb(st, f"junk_{tag}", [128, D], BF16)
            ss = sb(st, f"ss_{tag}", [128, n_tiles])
            xnb = [sb(st, f"xnb_{tag}{j}", [128, D], BF16) for j in range(2)]
            R_junk, R_ss = Res("junk"), Res("ss")
            R_xnb = [Res("xnb0"), Res("xnb1")]
            if ss_pre is None:
                for i in range(n_tiles):
                    P.op(ACT, lambda e: e.activation(out=junk[:], in_=src_ap_fn(i), func=AF.Square, accum_out=ss[:, i:i + 1]),
                         reads=[src_res_fn(i)], writes=[R_junk, R_ss])
                P.op(ACT, lambda e: e.activation(out=ss[:], in_=ss[:], func=AF.Sqrt, scale=1.0 / D, bias=EPS), reads=[R_ss], writes=[R_ss])
            else:
                P.op(ACT, lambda e: e.activation(out=ss[:], in_=ss_pre[0], func=AF.Sqrt, scale=1.0 / D, bias=EPS),
                     reads=[ss_pre[1]], writes=[R_ss])
            P.op(DVE, lambda e: e.reciprocal(out=ss[:], in_=ss[:]), reads=[R_ss], writes=[R_ss])
            for i in range(n_tiles):
                xb, rxb = xnb[i % 2], R_xnb[i % 2]
                P.op(DVE, lambda e: e.scalar_tensor_tensor(out=xb[:], in0=src_ap_fn(i), scalar=ss[:, i:i + 1], in1=gB[:],
                                                           op0=ALU.mult, op1=ALU.mult),
                     reads=[src_res_fn(i), R_ss, R_gB], writes=[rxb])
                pt, rpt = ptr[i % 2], R_ptr[i % 2]
                for k in range(8):
                    P.op(PE, lambda e: e.transpose(pt[:, k, :], xb[:, k * 128:(k + 1) * 128], ident[:]),
                         reads=[rxb, R_ident], writes=[rpt], inc=(k == 7))
                c0 = i * 128
                P.op(ACT, lambda e: e.copy(out=dstT[:, :, c0:c0 + 128], in_=pt[:]), reads=[rpt], writes=[dst_res_fn(i)])

        with contextlib.ExitStack() as mixst:
            xnT = sb(mixst, "xnT", [128, 8, T], BF16)
            R_xn = [Res(f"xn{i}") for i in range(T // 128)]
            R_xpad = Res("xpad")
            load_gain(0)
            cosT = sb(mixst, "cosT", [128, KV], BF16)
            sinT = sb(mixst, "sinT", [128, KV], BF16)
            ropec = sb(mixst, "ropec", [128, 3])
            R_cos, R_sin, R_ropec = Res("cos"), Res("sin"), Res("ropec")
            P.dma(SP, lambda e: e.dma_start(out=ropec[:], in_=ropec_d), writes=[R_ropec])

            rgscope = contextlib.ExitStack()
            chp = sb(rgscope, "chp", [128, 4, 12], side="right")
            cz = sb(rgscope, "cz", [128, 4, 2], side="right")
            czh = sb(rgscope, "czh", [128, 4, 2], side="right")
            chph = sb(rgscope, "chph", [128, 4, 4], side="right")
            gw = sb(rgscope, "gw", [128, 4, 2, 2, 128], BF16, side="right")
            cbt = sb(rgscope, "cbt", [128, 4], side="right")
            R_chp, R_cz, R_gw = Res("chp"), Res("cz"), Res("gw")
            R_cb = [Res(f"cb{c}") for c in range(4)]
            P.dma(SP, lambda e: e.dma_start(out=chp[:], in_=chp_d.rearrange("(ct p) n -> p ct n", p=128)), writes=[R_chp])
            P.op(POOL, lambda e: e.memset(gw[:], 0.0), writes=[R_gw])
            for ct in range(4):
                for z in range(2):
                    for ai, wd in enumerate((rga_d, rgx_d)):
                        for hb in range(2):
                            P.dma(POOL, lambda e: e.dma_start(out=gw[hb * 64:(hb + 1) * 64, ct, z, ai, hb * 64:(hb + 1) * 64],
                                                              in_=wd[z, 2 * ct + hb, :, :]), writes=[R_gw], par=True)
            P.op(ACT, lambda e: e.activation(out=cz[:], in_=chp[:, :, 10:12], func=AF.Exp, scale=-1.0), reads=[R_chp], writes=[R_cz])
            P.op(ACT, lambda e: e.activation(out=cz[:], in_=cz[:], func=AF.Ln, bias=1.0, scale=1.0), reads=[R_cz], writes=[R_cz])
            P.op(DVE, lambda e: e.tensor_scalar(out=czh[:], in0=cz[:], scalar1=-4.0, scalar2=None, op0=ALU.mult), reads=[R_cz], writes=[R_cz])
            P.op(DVE, lambda e: e.tensor_scalar(out=cz[:], in0=cz[:], scalar1=-8.0, scalar2=None, op0=ALU.mult), reads=[R_cz], writes=[R_cz])
            P.op(DVE, lambda e: e.tensor_scalar(out=chph[:], in0=chp[:, :, 6:10], scalar1=0.5, scalar2=None, op0=ALU.mult), reads=[R_chp], writes=[R_cz])


            with contextlib.ExitStack() as st:
                HK = KV // 2
                posi = sb(st, "posi", [128, KV], I32)
                R_posi = Res("posi")
                P.dma(SP, lambda e: e.dma_start(out=posi[:], in_=pos_d[0:1, 0:KV].broadcast_to([128, KV])), writes=[R_posi])
                ropebufs = []
                for which in range(2):
                    ropebufs.append((sb(st, f"ropey{which}", [128, HK]), sb(st, f"ropeki{which}", [128, HK], I32),
                                     sb(st, f"ropeg{which}", [128, HK]), Res("y"), Res("ki"), Res("g1")))

                def rope_thread():
                    EN = DVE
                    for which, (dst, rdst) in enumerate(((sinT, R_sin), (cosT, R_cos))):
                        y, ki, g1, R_y, R_ki, R_g1 = ropebufs[which]
                        for hh in range(2):
                            cs = slice(hh * HK, (hh + 1) * HK)
                            P.op(EN, lambda e: e.tensor_scalar(out=y[:], in0=posi[:, cs], scalar1=ropec[:, 0:1], scalar2=0.25 * which,
                                                               op0=ALU.mult, op1=ALU.add), reads=[R_posi, R_ropec], writes=[R_y])
                            yield
                            P.op(EN, lambda e: e.scalar_tensor_tensor(out=y[:], in0=posi[:, cs], scalar=ropec[:, 2:3], in1=y[:],
                                                                      op0=ALU.mult, op1=ALU.add), reads=[R_posi, R_ropec, R_y], writes=[R_y])
                            yield
                            P.op(EN, lambda e: e.tensor_copy(out=ki[:], in_=y[:]), reads=[R_y], writes=[R_ki])
                            yield
                            P.op(EN, lambda e: e.tensor_tensor(out=y[:], in0=y[:], in1=ki[:], op=ALU.subtract), reads=[R_y, R_ki], writes=[R_y])
                            yield
                            P.op(EN, lambda e: e.scalar_tensor_tensor(out=g1[:], in0=y[:], scalar=0.5, in1=y[:], op0=ALU.is_gt, op1=ALU.subtract),
                                 reads=[R_y], writes=[R_g1])
                            yield
                            P.op(EN, lambda e: e.scalar_tensor_tensor(out=y[:], in0=g1[:], scalar=0.5, in1=g1[:], op0=ALU.is_gt, op1=ALU.subtract),
                                 reads=[R_g1], writes=[R_y])
                            yield
                            P.op(EN, lambda e: e.tensor_scalar(out=y[:], in0=y[:], scalar1=0.4999999, scalar2=-0.4999999, op0=ALU.min, op1=ALU.max),
                                 reads=[R_y], writes=[R_y])
                            yield
                            P.op(ACT, lambda e: e.activation(out=y[:], in_=y[:], func=AF.Sin, scale=2.0 * math.pi), reads=[R_y], writes=[R_y])
                            yield
                            if which == 0:
                                P.op(EN, lambda e: e.tensor_scalar(out=dst[:, cs], in0=y[:], scalar1=ropec[:, 1:2], scalar2=None, op0=ALU.mult),
                                     reads=[R_y, R_ropec], writes=[rdst])
                            else:
                                P.op(EN, lambda e: e.tensor_copy(out=dst[:, cs], in_=y[:]), reads=[R_y], writes=[rdst])
                            yield

                xs = [sb(st, f"xs{j}", [128, D]) for j in range(3)]
                R_xs = [Res(f"xs{j}") for j in range(3)]
                ptr = [ps(st, f"ptrA{j}", [128, 8, 128], BF16) for j in range(2)]
                R_ptr = [Res("ptr0"), Res("ptr1")]

                def load_x(i):
                    P.dma(SP, lambda e: e.dma_start(out=xs[i % 3][:], in_=x_d[i * 128:(i + 1) * 128, :]), writes=[R_xs[i % 3]])
                run_threads({
                    "norm": (lambda: norm_transpose(st, "A", T // 128, lambda i: xs[i % 3][:], lambda i: R_xs[i % 3], xnT, 0,
                                                    lambda i: R_xn[i], ptr, R_ptr, load_fn=load_x), []),
                    "rope": (rope_thread, []),
                })
                P.barrier()
                ckpt_state["xnT"] = xnT
                checkpoint("A")

            def xn_res(tok0, tok1):
                rs = []
                if tok0 < 0:
                    rs.append(R_xpad)
                for i in range(max(tok0, 0) // 128, (max(tok1, 1) + 127) // 128):
                    rs.append(R_xn[i])
                return rs

            with contextlib.ExitStack() as st:
                HB = 2048
                U = [sb(st, f"U{j}", [128, HB + 4]) for j in range(2)]
                UC = [sb(st, f"UC{j}", [128, HB]) for j in range(2)]
                UCb = [sb(st, f"UCb{j}", [128, HB], BF16) for j in range(2)]
                R_UCb = [Res("UCb0"), Res("UCb1")]
                A = [sb(st, f"A{j}", [128, HB]) for j in range(2)]
                Bv = [sb(st, f"Bv{j}", [128, HB]) for j in range(2)]
                Hk = sb(st, "Hk", [128, HB])
                GY = sb(st, "GY", [128, OWN], BF16)
                wu = [sb(st, f"wu{j}", [128, 8, 128], BF16) for j in range(2)]
                wy = [sb(st, "wy0", [128, 8, 128], BF16)] * 2
                R_U = [Res("U0"), Res("U1")]
                R_UC = [Res("UC0"), Res("UC1")]
                R_A = [Res("A0"), Res("A1")]
                R_Bv = [Res("Bv0"), Res("Bv1")]
                R_Hk, R_GY = Res("Hk"), Res("GY")
                R_wu = [Res("wu0"), Res("wu1")]
                R_wy = [Res("wy0")] * 2
                pb = [ps(st, f"pbB{j}", [128, 512]) for j in range(7)]
                R_pb = [Res(f"pbB{j}") for j in range(7)]
                pbi = [0]

                def next_pb():
                    j = pbi[0] % 7
                    pbi[0] += 1
                    return pb[j], R_pb[j]

                def load_w_rg(ct):
                    c0 = 1536 + ct * 128
                    P.dma(POOL, lambda e: e.dma_start(out=wu[ct % 2][:], in_=w_in_d[:, c0:c0 + 128].rearrange("(k p) c -> p k c", p=128)),
                          writes=[R_wu[ct % 2]])
                    c1 = 2048 + ct * 128
                    P.dma(POOL, lambda e: e.dma_start(out=wy[ct % 2][:], in_=w_in_d[:, c1:c1 + 128].rearrange("(k p) c -> p k c", p=128)),
                          writes=[R_wy[ct % 2]])

                ritems = [(ct, hf) for ct in range(4) for hf in (1, 0)]

                def rg_P(i):
                    ct, hf = ritems[i]
                    s_ = i % 2
                    t0 = HB * hf
                    if hf == 1:
                        load_w_rg(ct)
                    wuc, rwu, wyc, rwy = wu[ct % 2], R_wu[ct % 2], wy[ct % 2], R_wy[ct % 2]
                    Us, rU = U[s_], R_U[s_]
                    for tb in range(HB // 512):
                        pt, rpt = next_pb()
                        c0 = t0 + tb * 512
                        for k in range(8):
                            P.op(PE, lambda e: e.matmul(pt[:], lhsT=wuc[:, k, :], rhs=xnT[:, k, c0:c0 + 512],
                                                        start=(k == 0), stop=(k == 7)),
                                 reads=[rwu] + xn_res(c0, c0 + 512), writes=[rpt], inc=(k == 7))
                        P.op(ACT, lambda e: e.copy(out=Us[:, 2 + tb * 512:2 + (tb + 1) * 512], in_=pt[:]), reads=[rpt], writes=[rU])
                        yield
                    pt, rpt = next_pb()
                    hb0 = HB - 128 if hf == 1 else HB
                    for k in range(8):
                        P.op(PE, lambda e: e.matmul(pt[:, 0:128], lhsT=wuc[:, k, :], rhs=xnT[:, k, hb0:hb0 + 128],
                                                    start=(k == 0), stop=(k == 7)),
                             reads=[rwu] + xn_res(hb0, hb0 + 128), writes=[rpt], inc=(k == 7))
                    if hf == 1:
                        P.op(ACT, lambda e: e.copy(out=Us[:, 0:2], in_=pt[:, 126:128]), reads=[rpt], writes=[rU])
                        P.op(POOL, lambda e: e.memset(Us[:, HB + 2:HB + 4], 0.0), writes=[rU])
                    else:
                        P.op(ACT, lambda e: e.copy(out=Us[:, HB + 2:HB + 4], in_=pt[:, 0:2]), reads=[rpt], writes=[rU])
                        P.op(POOL, lambda e: e.memset(Us[:, 0:2], 0.0), writes=[rU])
                        for tb in range(OWN // 512):
                            pt, rpt = next_pb()
                            for k in range(8):
                                P.op(PE, lambda e: e.matmul(pt[:], lhsT=wyc[:, k, :], rhs=xnT[:, k, tb * 512:(tb + 1) * 512],
                                                            start=(k == 0), stop=(k == 7)),
                                     reads=[rwy] + xn_res(tb * 512, (tb + 1) * 512), writes=[rpt], inc=(k == 7))
                            gsl = GY[:, tb * 512:(tb + 1) * 512]
                            P.op(ACT, lambda e: e.activation(out=gsl, in_=pt[:], func=AF.Identity, scale=0.5), reads=[rpt], writes=[R_GY])
                            P.op(ACT, lambda e: e.activation(out=Hk[:, tb * 512:(tb + 1) * 512], in_=pt[:], func=AF.Square, scale=0.5),
                                 reads=[rpt, R_Hk], writes=[R_Hk])
                            yield
                        P.op(DVE, lambda e: e.tensor_scalar(out=Hk[:], in0=Hk[:], scalar1=4.0 * 0.044715, scalar2=1.0, op0=ALU.mult, op1=ALU.add),
                             reads=[R_Hk], writes=[R_Hk])
                        yield
                        P.op(DVE, lambda e: e.tensor_tensor(out=Hk[:], in0=Hk[:], in1=GY[:], op=ALU.mult), reads=[R_Hk, R_GY], writes=[R_Hk])
                        yield
                        P.op(ACT, lambda e: e.activation(out=Hk[:], in_=Hk[:], func=AF.Tanh, scale=2.0 * math.sqrt(2.0 / math.pi)), reads=[R_Hk], writes=[R_Hk])
                        yield
                        P.op(DVE, lambda e: e.scalar_tensor_tensor(out=GY[:], in0=Hk[:], scalar=1.0, in1=GY[:], op0=ALU.add, op1=ALU.mult),
                             reads=[R_Hk, R_GY], writes=[R_GY])
                        yield
                    yield

                def rg_C(i):
                    ct, hf = ritems[i]
                    s_ = i % 2
                    Us, rU, UCs, rUC = U[s_], R_U[s_], UC[s_], R_UC[s_]
                    P.op(DVE, lambda e: e.tensor_scalar(out=UCs[:], in0=Us[:, 0:HB], scalar1=chp[:, ct, 0:1], scalar2=chp[:, ct, 5:6],
                                                        op0=ALU.mult, op1=ALU.add), reads=[rU, R_chp], writes=[rUC])
                    yield
                    for j in range(1, 5):
                        P.op(DVE, lambda e: e.scalar_tensor_tensor(out=UCs[:], in0=Us[:, j:j + HB], scalar=chp[:, ct, j:j + 1], in1=UCs[:],
                                                                   op0=ALU.mult, op1=ALU.add), reads=[rU, R_chp, rUC], writes=[rUC])
                        yield

                def rg_G(i, z):
                    ct, hf = ritems[i]
                    s_ = i % 2
                    P.op(ACT, lambda e: e.copy(out=UCb[s_][:], in_=UC[s_][:]), reads=[R_UC[s_]], writes=[R_UCb[s_]])
                    yield
                    for tb in range(HB // 512):
                        sl = slice(tb * 512, (tb + 1) * 512)
                        pr, rpr = next_pb()
                        P.op(PE, lambda e: e.matmul(pr[:], lhsT=gw[:, ct, z, 0, :], rhs=UCb[s_][:, sl], start=True, stop=True),
                             reads=[R_gw, R_UCb[s_]], writes=[rpr])
                        P.op(ACT, lambda e: e.activation(out=A[s_][:, sl], in_=pr[:], func=AF.Tanh, bias=chph[:, ct, z:z + 1], scale=0.5),
                             reads=[rpr, R_cz], writes=[R_A[s_]])
                        pi_, rpi = next_pb()
                        P.op(PE, lambda e: e.matmul(pi_[:], lhsT=gw[:, ct, z, 1, :], rhs=UCb[s_][:, sl], start=True, stop=True),
                             reads=[R_gw, R_UCb[s_]], writes=[rpi])
                        P.op(ACT, lambda e: e.activation(out=Bv[s_][:, sl], in_=pi_[:], func=AF.Tanh, bias=chph[:, ct, 2 + z:3 + z], scale=0.5),
                             reads=[rpi, R_cz], writes=[R_Bv[s_]])
                        yield

                def rg_E(i, z):
                    ct, hf = ritems[i]
                    s_ = i % 2
                    As, rA, Bs, rB, Us, rU, UCs, rUC = A[s_], R_A[s_], Bv[s_], R_Bv[s_], U[s_], R_U[s_], UC[s_], R_UC[s_]
                    Sx = Us[:, 0:HB]
                    P.op(ACT, lambda e: e.activation(out=Sx, in_=As[:], func=AF.Exp, scale=cz[:, ct, z:z + 1], bias=cz[:, ct, z:z + 1]),
                         reads=[rA, R_cz, rU], writes=[rU])
                    yield
                    P.op(ACT, lambda e: e.activation(out=As[:], in_=As[:], func=AF.Exp, scale=czh[:, ct, z:z + 1], bias=czh[:, ct, z:z + 1]),
                         reads=[rA, R_cz], writes=[rA])
                    yield
                    P.op(DVE, lambda e: e.scalar_tensor_tensor(out=Bs[:], in0=Bs[:], scalar=1.0, in1=UCs[:], op0=ALU.add, op1=ALU.mult),
                         reads=[rB, rUC], writes=[rB])
                    yield
                    P.op(ACT, lambda e: e.activation(out=Sx, in_=Sx, func=AF.Sqrt, scale=-0.25, bias=0.25), reads=[rU], writes=[rU])
                    yield
                    P.op(DVE, lambda e: e.tensor_tensor(out=Bs[:], in0=Bs[:], in1=Sx, op=ALU.mult), reads=[rB, rU], writes=[rB])
                    yield
                    if z == 1 and hf == 1:
                        P.op(DVE, lambda e: e.tensor_tensor_scan(out=Sx[:, ::-1], data0=As[:, ::-1], data1=Bs[:, ::-1], initial=0.0,
                                                                 op0=ALU.mult, op1=ALU.add), reads=[rA, rB, rU], writes=[rU])
                        yield
                        P.op(DVE, lambda e: e.tensor_copy(out=cbt[:, ct:ct + 1], in_=Us[:, 0:1]), reads=[rU], writes=[R_cb[ct]])
                    elif z == 1:
                        P.op(DVE, lambda e: e.tensor_tensor_scan(out=Hk[:, ::-1], data0=As[:, ::-1], data1=Bs[:, ::-1], initial=cbt[:, ct:ct + 1],
                                                                 op0=ALU.mult, op1=ALU.add), reads=[rA, rB, R_cb[ct], R_Hk], writes=[R_Hk])
                    else:
                        P.op(DVE, lambda e: e.tensor_tensor_scan(out=Sx, data0=As[:], data1=Bs[:], initial=0.0,
                                                                 op0=ALU.mult, op1=ALU.add), reads=[rA, rB, rU], writes=[rU])
                        yield
                        P.op(DVE, lambda e: e.tensor_tensor(out=Sx, in0=Sx, in1=Hk[:], op=ALU.add), reads=[rU, R_Hk], writes=[rU])
                        yield
                        P.op(DVE, lambda e: e.tensor_tensor(out=mixT[:, 4 + ct, :], in0=Sx, in1=GY[:], op=ALU.mult),
                             reads=[rU, R_GY], writes=[R_mixk[4 + ct]])

                threads = {}
                nI = len(ritems)
                lastG = [None]

                def gdeps(base):
                    return list(base)

                for i in range(nI):
                    lo = ritems[i][1] == 0
                    last_prev2 = None
                    if i >= 2:
                        last_prev2 = f"E{i - 2}_0" if ritems[i - 2][1] == 0 else f"E{i - 2}_1"
                    threads[f"P{i}"] = ((lambda i=i: rg_P(i)), ([last_prev2] if last_prev2 else []) + ([f"P{i - 1}"] if i > 0 else []))
                    threads[f"C{i}"] = ((lambda i=i: rg_C(i)), [f"P{i}"])
                    threads[f"G{i}_1"] = ((lambda i=i: rg_G(i, 1)), gdeps([f"C{i}"]))
                    lastG[0] = f"G{i}_1"
                    threads[f"E{i}_1"] = ((lambda i=i: rg_E(i, 1)), [f"G{i}_1"] + ([f"E{i - 1}_1"] if lo else []))
                    if lo:
                        threads[f"G{i}_0"] = ((lambda i=i: rg_G(i, 0)), gdeps([f"E{i}_1"]))
                        lastG[0] = f"G{i}_0"
                        threads[f"E{i}_0"] = ((lambda i=i: rg_E(i, 0)), [f"G{i}_0"])
                run_threads(threads)
                P.barrier()
                checkpoint("B")
            rgscope.close()

            with contextlib.ExitStack() as st:
                wo = sb(mixscope, "wo", [128, 8, D], BF16, side="right")
                R_wo = Res("wo")
                P.wo = (wo, R_wo)
                maskLU = sb(st, "maskLU", [128, 512], BF16)
                permT = sb(st, "permT", [128, 128], BF16)
                qb = [sb(st, f"qb{j}", [128, 512], BF16) for j in range(2)]
                R_mask, R_perm = Res("mask"), Res("perm")
                R_qb = [Res("qb0"), Res("qb1")]
                P.dma(POOL, lambda e: e.dma_start(out=maskLU[:], in_=mask_d), writes=[R_mask])
                P.dma(POOL, lambda e: e.dma_start(out=permT[:], in_=perm_d), writes=[R_perm])
                KrT = sb(st, "KrT", [128, PAD + KV], BF16)
                QX = [sb(st, f"QX{X}", [128, OWN], BF16) for X in range(2)]
                Vd = sb(st, "Vd", [128, 32, 128], BF16)
                accOD = sb(st, "accOD", [128, 2, OWN])
                accO = accOD[:, 0, :]
                accD = accOD[:, 1, :]
                t1 = [sb(st, f"t1_{j}", [128, 512]) for j in range(2)]
                t2 = [sb(st, f"t2_{j}", [128, 512]) for j in range(2)]
                Et = [[sb(st, f"E{X}{j}", [128, 512], BF16) for j in range(2)] for X in range(2)]
                wset = [{n: sb(st, f"{n}{j}", [128, 8, 128], BF16) for n in ("wq", "wk", "wv")} for j in range(2)]
                R_wset = [{n: Res(f"{n}{j}") for n in ("wq", "wk", "wv")} for j in range(2)]
                R_Kr, R_Qr, R_Vd, R_accO, R_accD = (Res(n) for n in ("Kr", "Qr", "Vd", "accO", "accD"))
                R_t1 = [Res("t10"), Res("t11")]
                R_t2 = [Res("t20"), Res("t21")]
                R_E = [[Res(f"E{X}{j}") for j in range(2)] for X in range(2)]
                pb = [ps(st, f"pbC{j}", [128, 512]) for j in range(7)]
                R_pb = [Res(f"pbC{j}") for j in range(7)]
                pvbf = ps(st, "pvbf", [128, 8, 128], BF16)
                R_pvbf = Res("pvbf")
                P.op(DVE, lambda e: e.memset(KrT[:, 0:PAD], 0.0), writes=[R_Kr])
                P.op(DVE, lambda e: e.memset(QX[0][64:128, :], 0.0), writes=[R_Qr])
                P.op(DVE, lambda e: e.memset(QX[1][0:64, :], 0.0), writes=[R_Qr])

                def wsrc(c0, n=128):
                    return w_in_d[:, c0:c0 + n].rearrange("(k p) c -> p k c", p=128)

                def load_pair_weights(hp):
                    cq, ck, cv = hp * 128, 512 + hp * 128, 1024 + hp * 128
                    for n, c0 in (("wk", ck), ("wq", cq), ("wv", cv)):
                        P.dma(POOL, lambda e: e.dma_start(out=wset[hp % 2][n][:], in_=wsrc(c0)), writes=[R_wset[hp % 2][n]])

                pbi = [0]

                def next_pb(lo=0, hi=3):
                    j = lo + pbi[0] % (hi - lo)
                    pbi[0] += 1
                    return pb[j], R_pb[j]

                VT = sb(st, "VT", [128, PAD + KV], BF16)
                R_VT = Res("VT")
                P.op(DVE, lambda e: e.memset(VT[:, 0:PAD], 0.0), writes=[R_VT])
                E4 = [Et[0][0], Et[0][1], Et[1][0], Et[1][1]]
                R_E4 = [R_E[0][0], R_E[0][1], R_E[1][0], R_E[1][1]]
                LOOK = 2
                sidx = [0]
                odx = [0, 0]

                load_pair_weights(0)
                P.dma(POOL, lambda e: e.dma_start(out=wo[:], in_=w_out_d.rearrange("(k p) c -> p k c", p=128)), writes=[R_wo])
                for hp in range(4):
                    if hp + 1 < 4:
                        load_pair_weights(hp + 1)
                    wq, wk, wv = (wset[hp % 2][n] for n in ("wq", "wk", "wv"))
                    R_w = R_wset[hp % 2]
                    pj = [0]

                    def rope_s1(tb, wa, rwa):
                        xr = xn_res(tb * 512, (tb + 1) * 512)
                        pa, rpa = next_pb(0, 7)
                        for k in range(8):
                            P.op(PE, lambda e: e.matmul(pa[:], lhsT=wa[:, k, :], rhs=xnT[:, k, tb * 512:(tb + 1) * 512],
                                                        start=(k == 0), stop=(k == 7)), reads=[rwa] + xr, writes=[rpa], inc=(k == 7))
                        u = pj[0] % 2
                        pj[0] += 1
                        P.op(ACT, lambda e: e.copy(out=qb[u][:], in_=pa[:]), reads=[rpa], writes=[R_qb[u]])
                        return (pa, rpa, u)

                    def rope_s2(tb, isq, pa, rpa, u):
                        sl = slice(tb * 512, (tb + 1) * 512)
                        pbb, rpb_ = next_pb(0, 7)
                        P.op(PE, lambda e: e.matmul(pbb[:], lhsT=permT[:], rhs=qb[u][:], start=True, stop=True),
                             reads=[R_perm, R_qb[u]], writes=[rpb_])
                        P.op(DVE, lambda e: e.tensor_tensor(out=t1[u][:], in0=pa[:], in1=cosT[:, sl], op=ALU.mult),
                             reads=[rpa, R_cos], writes=[R_t1[u]])
                        P.op(DVE, lambda e: e.tensor_tensor(out=t2[u][:], in0=pbb[:], in1=sinT[:, sl], op=ALU.mult),
                             reads=[rpb_, R_sin], writes=[R_t2[u]])
                        if not isq:
                            P.op(POOL, lambda e: e.tensor_tensor(out=KrT[:, PAD + tb * 512:PAD + (tb + 1) * 512], in0=t1[u][:], in1=t2[u][:], op=ALU.add),
                                 reads=[R_t1[u], R_t2[u]], writes=[R_Kr])
                        else:
                            for X in range(2):
                                hsx = slice(64 * X, 64 * X + 64)
                                P.op(POOL, lambda e: e.tensor_tensor(out=QX[X][hsx, sl], in0=t1[u][hsx, :], in1=t2[u][hsx, :], op=ALU.add),
                                     reads=[R_t1[u], R_t2[u]], writes=[R_Qr])

                    rjobs = []
                    for tb in range(KV // 512):
                        rjobs.append((tb, False))
                        if tb < OWN // 512:
                            rjobs.append((tb, True))
                    prev = None
                    for ji, (tb, isq) in enumerate(rjobs):
                        cur = (tb, isq) + rope_s1(tb, wq if isq else wk, R_w["wq"] if isq else R_w["wk"])
                        if prev is not None:
                            rope_s2(*prev)
                        prev = cur
                        if not isq:
                            xr = xn_res(tb * 512, (tb + 1) * 512)
                            pv, rpv = next_pb(0, 7)
                            for k in range(8):
                                P.op(PE, lambda e: e.matmul(pv[:], lhsT=wv[:, k, :], rhs=xnT[:, k, tb * 512:(tb + 1) * 512],
                                                            start=(k == 0), stop=(k == 7)), reads=[R_w["wv"]] + xr, writes=[rpv], inc=(k == 7))
                            P.op(ACT, lambda e: e.copy(out=VT[:, PAD + tb * 512:PAD + (tb + 1) * 512], in_=pv[:]), reads=[rpv], writes=[R_VT])
                    rope_s2(*prev)

                    def key_tok0(d, r, i):
                        return d * (128 * i - 64) + r

                    def build_V(d):
                        nb = 16 // d
                        ntile = d * (nb + 1)
                        for t0 in range(0, ntile, 8):
                            rpv = R_pvbf
                            pvb = pvbf[:].rearrange("p a b -> p (a b)")
                            n = min(8, ntile - t0)
                            for c in range(n):
                                r, i = divmod(t0 + c, nb + 1)
                                a0 = PAD + key_tok0(d, r, i)
                                P.op(PE, lambda e: e.transpose(pvb[:, c * 128:(c + 1) * 128], VT[:, a0:a0 + 127 * d + 1:d], ident[:]),
                                     reads=[R_VT, R_ident], writes=[rpv], inc=(c == n - 1))
                            P.op(ACT, lambda e: e.copy(out=Vd[:, t0:t0 + n, :], in_=pvb[:, 0:n * 128].rearrange("p (a b) -> p a b", b=128)),
                                 reads=[rpv], writes=[R_Vd])

                    def emit_S(it):
                        pi_, d, nb, u, X, qbs = it
                        hs = slice(64 * X, 64 * X + 64)
                        j = sidx[0] % 4
                        js = sidx[0] % 3
                        sidx[0] += 1
                        psb, rps = pb[js], R_pb[js]
                        Ex, rEx = E4[j], R_E4[j]
                        P.op(PE, lambda e: e.matmul(psb[:], lhsT=ident[:], rhs=maskLU[:], start=True, stop=True),
                             reads=[R_ident, R_mask], writes=[rps], inc=False)
                        shared = (qbs[0][0] == qbs[1][0] and qbs[1][1] == qbs[0][1] + 1)
                        if shared:
                            r, m = qbs[0]
                            q0 = d * 128 * m + r
                            plan = [(m, 0, 0, 128), (m + 1, 0, 128, 256), (m + 2, 128, 384, 128)]
                            for pi2, (kt, qoff, c0, n) in enumerate(plan):
                                a0 = PAD + key_tok0(d, r, kt)
                                qs = q0 + d * qoff
                                P.op(PE, lambda e: e.matmul(psb[:, c0:c0 + n], lhsT=KrT[:, a0:a0 + 127 * d + 1:d],
                                                            rhs=QX[X][:, qs:qs + (n - 1) * d + 1:d], start=False, stop=True,
                                                            skip_group_check=True),
                                     reads=[R_Kr, R_Qr], writes=[rps], inc=(pi2 == 2))
                        else:
                            for uu in range(2):
                                r, m = qbs[uu]
                                q0 = d * 128 * m + r
                                for ti in range(2):
                                    a0 = PAD + key_tok0(d, r, m + ti)
                                    c0 = uu * 256 + ti * 128
                                    P.op(PE, lambda e: e.matmul(psb[:, c0:c0 + 128], lhsT=KrT[:, a0:a0 + 127 * d + 1:d],
                                                                rhs=QX[X][:, q0:q0 + 127 * d + 1:d], start=False, stop=True,
                                                                skip_group_check=True),
                                         reads=[R_Kr, R_Qr], writes=[rps], inc=(uu == 1 and ti == 1))
                        P.op(ACT, lambda e: e.activation(out=Ex[:], in_=psb[:], func=AF.Exp, scale=0.125), reads=[rps], writes=[rEx])
                        return (Ex, rEx)

                    def emit_PV(it, Ex, rEx):
                        pi_, d, nb, u, X, qbs = it
                        hs = slice(64 * X, 64 * X + 64)
                        jb = 3 + 2 * X + (odx[X] % 2)
                        odx[X] += 1
                        pOD, rOD = pb[jb], R_pb[jb]
                        shared = (qbs[0][0] == qbs[1][0] and qbs[1][1] == qbs[0][1] + 1)
                        if shared:
                            r, m = qbs[0]
                            vt0 = r * (nb + 1) + m
                            plan = [(vt0 + 1, 128, 256, 0, True), (vt0, 0, 128, 0, False), (vt0 + 2, 384, 128, 128, False)]
                            for (vt, c0, n, oc, first) in plan:
                                P.op(PE, lambda e: e.matmul(pOD[:, oc:oc + n], lhsT=Vd[:, vt, :], rhs=Ex[:, c0:c0 + n],
                                                            start=first, stop=True, skip_group_check=True),
                                     reads=[R_Vd, rEx], writes=[rOD], inc=False)
                            for pi2, (vt, c0, n, oc, first) in enumerate(plan):
                                on = onespad_bf if (vt == vt0 and m == 0) else ones_bf
                                P.op(PE, lambda e: e.matmul(pOD[:, 256 + oc:256 + oc + n], lhsT=on[:], rhs=Ex[:, c0:c0 + n],
                                                            start=first, stop=True, skip_group_check=True),
                                     reads=[R_ones, rEx], writes=[rOD], inc=(pi2 == 2))
                        else:
                            for uu in range(2):
                                r, m = qbs[uu]
                                for ti in range(2):
                                    c0 = uu * 256 + ti * 128
                                    vt = r * (nb + 1) + m + ti
                                    P.op(PE, lambda e: e.matmul(pOD[:, uu * 128:(uu + 1) * 128], lhsT=Vd[:, vt, :], rhs=Ex[:, c0:c0 + 128],
                                                                start=(ti == 0), stop=(ti == 1)),
                                         reads=[R_Vd, rEx], writes=[rOD], inc=False)
                                for ti in range(2):
                                    c0 = uu * 256 + ti * 128
                                    on = onespad_bf if (m + ti) == 0 else ones_bf
                                    P.op(PE, lambda e: e.matmul(pOD[:, 256 + uu * 128:256 + (uu + 1) * 128], lhsT=on[:], rhs=Ex[:, c0:c0 + 128],
                                                                start=(ti == 0), stop=(ti == 1)),
                                         reads=[R_ones, rEx], writes=[rOD], inc=(uu == 1 and ti == 1))
                        r, m = qbs[0]
                        if d == 1:
                            dst = accOD[hs, :, m * 128:(m + 2) * 128]
                            src = pOD[hs, :].rearrange("p (t c) -> p t c", t=2)
                        elif d == 4:
                            b0 = r + 512 * m
                            dst = accOD[hs, :, b0:b0 + 255 * 4 + 1:4]
                            src = pOD[hs, :].rearrange("p (t c) -> p t c", t=2)
                        else:
                            dst = accOD[:].rearrange("p t (j r) -> p t r j", r=16)[hs, :, r:r + 2, :]
                            src = pOD[hs, :].rearrange("p (t r j) -> p t r j", t=2, j=128)
                        if pi_ == 0:
                            P.op(DVE, lambda e: e.tensor_copy(out=dst, in_=src), reads=[rOD], writes=[R_accO, R_accD])
                        else:
                            P.op(DVE, lambda e: e.tensor_tensor(out=dst, in0=dst, in1=src, op=ALU.add), reads=[rOD, R_accO, R_accD],
                                 writes=[R_accO, R_accD])

                    items = []
                    for pi_, d in enumerate(PATTERNS):
                        nb = 16 // d
                        qblocks = [(r, m) for r in range(d) for m in range(nb)]
                        for u in range(8):
                            for X in range(2):
                                items.append((pi_, d, nb, u, X, (qblocks[2 * u], qblocks[2 * u + 1])))
                    pending = []
                    cur = None
                    for it in items:
                        if it[0] != cur:
                            while pending:
                                emit_PV(*pending.pop(0))
                            build_V(it[1])
                            cur = it[0]
                        ex = emit_S(it)
                        pending.append((it,) + ex)
                        if len(pending) > LOOK:
                            emit_PV(*pending.pop(0))
                    while pending:
                        emit_PV(*pending.pop(0))
                    P.op(ACT, lambda e: e.activation(out=accD, in_=accD, func=AF.Ln), reads=[R_accD], writes=[R_accD])
                    P.op(ACT, lambda e: e.activation(out=accD, in_=accD, func=AF.Exp, scale=-1.0), reads=[R_accD], writes=[R_accD])
                    P.op(DVE, lambda e: e.tensor_tensor(out=mixT[:, hp, :], in0=accO, in1=accD, op=ALU.mult),
                         reads=[R_accO, R_accD], writes=[R_mixk[hp]])
                P.barrier()
                checkpoint("C")

        if debug:
            P.dma(SP, lambda e: e.dma_start(out=dbg["mixT"], in_=mixT[:]), reads=R_mixk, sres=R_mixT)
            P.barrier()

        h = sb(top, "h", [128, NT_OWN, D])
        R_h = [Res(f"h{i}") for i in range(NT_OWN)]
        ssN = sb(top, "ssN", [128, 3, NT_OWN])
        junkN = sb(top, "junkN", [128, D], BF16)
        R_ssN = [Res(f"ssN{j}") for j in range(3)]
        R_junkN = Res("junkN")

        wcq = sb(top, "wcq", [128, 8, 512], BF16)
        wck = sb(top, "wck", [128, 8, 512], BF16)
        wcv = sb(top, "wcv", [128, 8, 512], BF16)
        wco = sb(top, "wco", [128, 4, D], BF16)
        ms = [sb(top, f"ms{j}", [128, D]) for j in range(2)]
        R_wc = {n: Res(n) for n in ("wcq", "wck", "wcv", "wco")}
        R_ms = [Res("ms0"), Res("ms1")]
        for (wt, wd, n) in ((wck, w_ck_d, "wck"), (wcv, w_cv_d, "wcv"), (wcq, w_cq_d, "wcq"), (wco, w_co_d, "wco")):
            P.dma(POOL, lambda e: e.dma_start(out=wt[:], in_=wd.rearrange("(k p) c -> p k c", p=128)), writes=[R_wc[n]])
        for j in range(2):
            P.dma(SP, lambda e: e.dma_start(out=ms[j][:], in_=mem_d[j * 128:(j + 1) * 128, :]), writes=[R_ms[j]])

        def emit_sumsq(j, i):
            P.op(ACT, lambda e: e.activation(out=junkN[:], in_=h[:, i, :], func=AF.Square, accum_out=ssN[:, j, i:i + 1]),
                 reads=[R_h[i]], writes=[R_junkN, R_ssN[j]])

        def accumulate_proj(st, tag, lhsT_fn, lhs_res, nk, w_sb, R_wsb, first_x=False, banks=None, ss_next=None):
            if banks is None:
                pbs = [ps(st, f"pb{tag}{j}", [128, 512]) for j in range(4)]
                R_pbs = [Res(f"pb{tag}{j}") for j in range(4)]
            else:
                pbs, R_pbs = banks
            xs = R_xs = None
            if first_x:
                xs = [sb(st, f"xs{tag}{j}", [128, D]) for j in range(2)]
                R_xs = [Res("xsa"), Res("xsb")]
            c = 0
            for i in range(NT_OWN):
                if first_x:
                    P.dma(SP, lambda e: e.dma_start(out=xs[i % 2][:], in_=x_d[i * 128:(i + 1) * 128, :]), writes=[R_xs[i % 2]])
                for nh in range(2):
                    pt, rpt = pbs[c % 4], R_pbs[c % 4]
                    c += 1
                    for k in range(nk):
                        P.op(PE, lambda e: e.matmul(pt[:], lhsT=lhsT_fn(k, i), rhs=w_sb[:, k, nh * 512:(nh + 1) * 512],
                                                    start=(k == 0), stop=(k == nk - 1)),
                             reads=[lhs_res(k), R_wsb], writes=[rpt], inc=(k == nk - 1))
                    hsl = h[:, i, nh * 512:(nh + 1) * 512]
                    if first_x:
                        P.op(DVE, lambda e: e.tensor_tensor(out=hsl, in0=pt[:], in1=xs[i % 2][:, nh * 512:(nh + 1) * 512], op=ALU.add),
                             reads=[rpt, R_xs[i % 2]], writes=[R_h[i]])
                    else:
                        P.op(DVE, lambda e: e.tensor_tensor(out=hsl, in0=pt[:], in1=hsl, op=ALU.add), reads=[rpt, R_h[i]], writes=[R_h[i]])
                if ss_next is not None:
                    emit_sumsq(ss_next, i)

        with contextlib.ExitStack() as st:
            wo, R_wo = P.wo
            accumulate_proj(st, "D", lambda k, i: mixT[:, k, i * 128:(i + 1) * 128], lambda k: R_mixk[k], 8, wo, R_wo, first_x=True, ss_next=0)
            P.barrier()
        mixscope.close()
        if debug:
            P.dma(SP, lambda e: e.dma_start(out=dbg["h1"], in_=h[:]), reads=R_h, sres=R_h[0])
            P.barrier()

        with contextlib.ExitStack() as st:
            hnT = sb(st, "hnT", [128, 8, OWN], BF16)
            R_hn = [Res(f"hn{i}") for i in range(NT_OWN)]
            with contextlib.ExitStack() as st1:
                memnT = sb(st1, "memnT", [128, 8, 256], BF16)
                KcT = sb(st1, "KcT", [128, 4, 256], BF16)
                Vc = sb(st1, "Vc", [128, 2, 512], BF16)
                OcT = sb(st1, "OcT", [128, 4, OWN], BF16)
                R_memn, R_Kc, R_Vc = Res("memn"), Res("Kc"), Res("Vc")
                R_Oc = [Res(f"Oc{k}") for k in range(4)]
                with contextlib.ExitStack() as st2:
                    ptr = [ps(st2, f"ptrE{j}", [128, 8, 128], BF16) for j in range(2)]
                    R_ptr = [Res("ptrE0"), Res("ptrE1")]
                    load_gain(2)
                    for _ in norm_transpose(st2, "Em", 2, lambda i: ms[i][:], lambda i: R_ms[i], memnT, 0, lambda i: R_memn, ptr, R_ptr):
                        pass
                    load_gain(1)
                    norm_transpose_batched(st2, "Eh", NT_OWN, lambda i: h[:, i, :], lambda i: R_h[i], hnT, lambda i: R_hn[i], ptr, R_ptr,
                                           ss_pre=(ssN[:, 0, :], R_ssN[0]))
                    P.barrier()
                pb = [ps(st1, f"pbE{j}", [128, 512]) for j in range(8)]
                R_pb = [Res(f"pbE{j}") for j in range(8)]
                qt = [sb(st1, f"qt{j}", [128, 512], BF16) for j in range(2)]
                R_qt = [Res("qt0"), Res("qt1")]
                Ec = [[sb(st1, f"Ec{j}{m}", [128, 512], BF16) for m in range(2)] for j in range(2)]
                R_Ec = [[Res(f"Ec{j}{m}") for m in range(2)] for j in range(2)]
                rD = [sb(st1, f"rDc{j}", [128, 512]) for j in range(2)]
                R_rD = [Res("rD0"), Res("rD1")]
                for hc in range(4):
                    pt, rpt = pb[hc % 2], R_pb[hc % 2]
                    for k in range(8):
                        P.op(PE, lambda e: e.matmul(pt[:, 0:256], lhsT=wck[:, k, hc * 128:(hc + 1) * 128], rhs=memnT[:, k, :],
                                                    start=(k == 0), stop=(k == 7)), reads=[R_wc["wck"], R_memn], writes=[rpt], inc=(k == 7))
                    P.op(ACT, lambda e: e.copy(out=KcT[:, hc, :], in_=pt[:, 0:256]), reads=[rpt], writes=[R_Kc])
                for mt in range(2):
                    pt, rpt = pb[2 + mt], R_pb[2 + mt]
                    for k in range(8):
                        P.op(PE, lambda e: e.matmul(pt[:], lhsT=memnT[:, k, mt * 128:(mt + 1) * 128], rhs=wcv[:, k, :],
                                                    start=(k == 0), stop=(k == 7)), reads=[R_wc["wcv"], R_memn], writes=[rpt], inc=(k == 7))
                    P.op(ACT, lambda e: e.copy(out=Vc[:, mt, :], in_=pt[:]), reads=[rpt], writes=[R_Vc])
                def cross_a(it, tb, hc):
                    sl = slice(tb * 512, (tb + 1) * 512)
                    hr = [R_hn[i] for i in range(tb * 4, tb * 4 + 4)]
                    j = it % 2
                    pq, rpq = pb[0 + j], R_pb[0 + j]
                    for k in range(8):
                        P.op(PE, lambda e: e.matmul(pq[:], lhsT=wcq[:, k, hc * 128:(hc + 1) * 128], rhs=hnT[:, k, sl],
                                                    start=(k == 0), stop=(k == 7)), reads=[R_wc["wcq"]] + hr, writes=[rpq], inc=(k == 7))
                    P.op(DVE, lambda e: e.tensor_copy(out=qt[j][:], in_=pq[:]), reads=[rpq], writes=[R_qt[j]])
                    for mt in range(2):
                        pS, rS = pb[2 + 2 * j + mt], R_pb[2 + 2 * j + mt]
                        P.op(PE, lambda e: e.matmul(pS[:], lhsT=KcT[:, hc, mt * 128:(mt + 1) * 128], rhs=qt[j][:], start=True, stop=True),
                             reads=[R_Kc, R_qt[j]], writes=[rS])
                        P.op(ACT, lambda e: e.activation(out=Ec[j][mt][:], in_=pS[:], func=AF.Exp, scale=128.0 ** -0.5),
                             reads=[rS], writes=[R_Ec[j][mt]])

                def cross_b(it, tb, hc):
                    sl = slice(tb * 512, (tb + 1) * 512)
                    j = it % 2
                    pO, rO = pb[6], R_pb[6]
                    pD, rDp = pb[7], R_pb[7]
                    for mt in range(2):
                        P.op(PE, lambda e: e.matmul(pO[:], lhsT=Vc[:, mt, hc * 128:(hc + 1) * 128], rhs=Ec[j][mt][:],
                                                    start=(mt == 0), stop=(mt == 1)), reads=[R_Vc, R_Ec[j][mt]], writes=[rO], inc=(mt == 1))
                    for mt in range(2):
                        P.op(PE, lambda e: e.matmul(pD[:], lhsT=ones_bf[:], rhs=Ec[j][mt][:],
                                                    start=(mt == 0), stop=(mt == 1)), reads=[R_ones, R_Ec[j][mt]], writes=[rDp], inc=(mt == 1))
                    P.op(ACT, lambda e: e.activation(out=rD[j][:], in_=pD[:], func=AF.Ln), reads=[rDp], writes=[R_rD[j]])
                    P.op(ACT, lambda e: e.activation(out=rD[j][:], in_=rD[j][:], func=AF.Exp, scale=-1.0), reads=[R_rD[j]], writes=[R_rD[j]])
                    P.op(DVE, lambda e: e.tensor_tensor(out=OcT[:, hc, sl], in0=pO[:], in1=rD[j][:], op=ALU.mult),
                         reads=[rO, R_rD[j]], writes=[R_Oc[hc]])

                citems = [(tb, hc) for tb in range(OWN // 512) for hc in range(4)]
                cross_a(0, *citems[0])
                for it in range(len(citems)):
                    if it + 1 < len(citems):
                        cross_a(it + 1, *citems[it + 1])
                    cross_b(it, *citems[it])
                P.barrier()
                with contextlib.ExitStack() as st3:
                    accumulate_proj(st3, "E", lambda k, i: OcT[:, k, i * 128:(i + 1) * 128], lambda k: R_Oc[k], 4, wco, R_wc["wco"], banks=(pb[0:4], R_pb[0:4]), ss_next=1)
                    P.barrier()
            if debug:
                P.dma(SP, lambda e: e.dma_start(out=dbg["h2"], in_=h[:]), reads=R_h, sres=R_h[0])
                P.barrier()

            with contextlib.ExitStack() as st1:
                NG = 8
                w1g = [sb(st1, f"w1g{j}", [128, 8, 512], BF16) for j in range(2)]
                w2g = [sb(st1, f"w2g{j}", [128, 4, D], BF16) for j in range(2)]
                R_w1g = [Res("w1g0"), Res("w1g1")]
                R_w2g = [Res("w2g0"), Res("w2g1")]

                def load_mlp_w(g):
                    j = g % 2
                    P.dma(POOL, lambda e: e.dma_start(out=w1g[j][:], in_=w1_d[:, g * 512:(g + 1) * 512].rearrange("(k p) c -> p k c", p=128)),
                          writes=[R_w1g[j]])
                    P.dma(POOL, lambda e: e.dma_start(out=w2g[j][:], in_=w2_d[g * 512:(g + 1) * 512, :].rearrange("(k p) c -> p k c", p=128)),
                          writes=[R_w2g[j]])

                load_mlp_w(0)
                with contextlib.ExitStack() as st2:
                    ptr = [ps(st2, f"ptrF{j}", [128, 8, 128], BF16) for j in range(2)]
                    R_ptr = [Res("ptrF0"), Res("ptrF1")]
                    load_gain(3)
                    norm_transpose_batched(st2, "F", NT_OWN, lambda i: h[:, i, :], lambda i: R_h[i], hnT, lambda i: R_hn[i], ptr, R_ptr,
                                           ss_pre=(ssN[:, 1, :], R_ssN[1]))
                    P.barrier()
                hid = sb(st1, "hid", [128, 4, OWN], BF16)
                R_hid = [[Res(f"hid{ft}_{tb}") for tb in range(4)] for ft in range(4)]
                rl = [sb(st1, f"rl{j}", [128, 512], BF16) for j in range(2)]
                R_rl = [Res("rl0"), Res("rl1")]
                pb = [ps(st1, f"pbF{j}", [128, 512]) for j in range(8)]
                R_pb = [Res(f"pbF{j}") for j in range(8)]

                c1 = 0
                c2 = 0
                for g in range(NG):
                    if g + 1 < NG:
                        load_mlp_w(g + 1)
                    j = g % 2
                    for tb in range(4):
                        sl = slice(tb * 512, (tb + 1) * 512)
                        hr = [R_hn[i] for i in range(tb * 4, tb * 4 + 4)]
                        for ft in range(4):
                            pt, rpt = pb[c1 % 4], R_pb[c1 % 4]
                            u = c1 % 2
                            c1 += 1
                            for k in range(8):
                                P.op(PE, lambda e: e.matmul(pt[:], lhsT=w1g[j][:, k, ft * 128:(ft + 1) * 128], rhs=hnT[:, k, sl],
                                                            start=(k == 0), stop=(k == 7)), reads=[R_w1g[j]] + hr, writes=[rpt], inc=(k == 7))
                            P.op(ACT, lambda e: e.activation(out=rl[u][:], in_=pt[:], func=AF.Relu), reads=[rpt], writes=[R_rl[u]])
                            P.op(POOL, lambda e: e.tensor_tensor(out=hid[:, ft, sl], in0=rl[u][:], in1=rl[u][:], op=ALU.mult),
                                 reads=[R_rl[u]], writes=[R_hid[ft][tb]])
                    for i in range(NT_OWN):
                        for nh in range(2):
                            pt, rpt = pb[4 + c2 % 4], R_pb[4 + c2 % 4]
                            c2 += 1
                            for ft in range(4):
                                P.op(PE, lambda e: e.matmul(pt[:], lhsT=hid[:, ft, i * 128:(i + 1) * 128], rhs=w2g[j][:, ft, nh * 512:(nh + 1) * 512],
                                                            start=(ft == 0), stop=(ft == 3)),
                                     reads=[R_hid[ft][i // 4], R_w2g[j]], writes=[rpt], inc=(ft == 3))
                            hsl = h[:, i, nh * 512:(nh + 1) * 512]
                            P.op(DVE, lambda e: e.tensor_tensor(out=hsl, in0=pt[:], in1=hsl, op=ALU.add), reads=[rpt, R_h[i]], writes=[R_h[i]])
                        if g == NG - 1:
                            emit_sumsq(2, i)
                P.barrier()

        with contextlib.ExitStack() as st:
            junk = sb(st, "junkG", [128, D], BF16)
            ss = sb(st, "ssG", [128, NT_OWN])
            ob = [sb(st, f"ob{j}", [128, D]) for j in range(2)]
            R_junk, R_ss = Res("junkG"), Res("ssG")
            R_ob = [Res("ob0"), Res("ob1")]
            load_gain(4)
            finals = []
            P.op(ACT, lambda e: e.activation(out=ss[:], in_=ssN[:, 2, :], func=AF.Sqrt, scale=1.0 / D, bias=EPS), reads=[R_ssN[2]], writes=[R_ss])
            P.op(DVE, lambda e: e.reciprocal(out=ss[:], in_=ss[:]), reads=[R_ss], writes=[R_ss])
            for i in range(NT_OWN):
                P.op(DVE, lambda e: e.scalar_tensor_tensor(out=ob[i % 2][:], in0=h[:, i, :], scalar=ss[:, i:i + 1], in1=gB[:],
                                                           op0=ALU.mult, op1=ALU.mult), reads=[R_h[i], R_ss, R_gB], writes=[R_ob[i % 2]])
                ev = P.dma(SP, lambda e: e.dma_start(out=out_d[i * 128:(i + 1) * 128, :], in_=ob[i % 2][:]), reads=[R_ob[i % 2]])
                finals.append(ev)
            P.final_wait(P._compress([f for f in finals if f is not None]))


def _consts():
    ident = np.eye(128, dtype=np.float32)
    i = np.arange(128)[:, None]
    j = np.arange(128)[None, :]
    L = np.where(j <= i, 0.0, -30000.0).astype(np.float32)
    U = np.where(j >= i, 0.0, -30000.0).astype(np.float32)
    mask = np.concatenate([L, U, L, U], axis=1)
    p = np.arange(128)
    c64 = (10000.0 ** (-np.arange(32, dtype=np.float64) / 32.0)) / (2.0 * np.pi)
    c_hi = c64.astype(np.float32)
    c_lo = (c64 - c_hi.astype(np.float64)).astype(np.float32)
    ropec = np.zeros((128, 3), np.float32)
    ropec[:, 0] = c_hi[p % 32]
    ropec[:, 1] = np.where((p % 64) < 32, -1.0, 1.0)
    ropec[:, 2] = c_lo[p % 32]
    m = np.arange(128)
    sw = (m // 64) * 64 + ((m % 64) + 32) % 64
    perm = np.zeros((128, 128), np.float32)
    perm[sw, m] = 1.0
    return ident, mask, ropec, perm


def make_in_maps(inputs):
    f = lambda a: np.ascontiguousarray(np.asarray(a))
    x = f(inputs["x"]); mem = f(inputs["mem"]); pos = f(inputs["positions"])
    ident, mask, ropec, perm = _consts()
    gains = np.stack([f(inputs["norm_mix_g"])[0], f(inputs["norm_cross_g"])[0], f(inputs["norm_mem_g"])[0],
                      f(inputs["norm_mlp_g"])[0], f(inputs["norm_final_g"])], axis=0).astype(np.float32)
    conv_w = f(inputs["conv_w"])[0]
    conv_b = f(inputs["conv_b"])[0]
    a_b = f(inputs["rg_a_b"])[0].reshape(2, 512)
    x_b = f(inputs["rg_x_b"])[0].reshape(2, 512)
    lam = f(inputs["rg_lambda"])[0]
    a_w = f(inputs["rg_a_w"])[0]
    x_w = f(inputs["rg_x_w"])[0]
    zero = np.zeros((1, 512), np.float32)
    shared = {
        "gains": gains, "w_in": f(inputs["w_in"])[0], "w_out": f(inputs["w_out"])[0],
        "w_cq": f(inputs["w_cq"])[0], "w_ck": f(inputs["w_ck"])[0], "w_cv": f(inputs["w_cv"])[0], "w_co": f(inputs["w_co"])[0],
        "w_mlp_in": f(inputs["w_mlp_in"])[0], "w_mlp_out": f(inputs["w_mlp_out"])[0],
        "c_ident": ident, "c_mask": mask, "c_rope": ropec, "c_perm": perm,
    }
    in_maps = []
    for b in range(4):
        for rev in range(2):
            m = dict(shared)
            if rev == 0:
                m["x"] = x[b]
                m["pos"] = pos[b][None, :].astype(np.int32)
                conv5 = np.concatenate([conv_w, zero], axis=0)
                z = [0, 1]
            else:
                m["x"] = np.ascontiguousarray(x[b][::-1])
                m["pos"] = np.ascontiguousarray(pos[b][::-1])[None, :].astype(np.int32)
                conv5 = np.concatenate([zero, conv_w[::-1]], axis=0)
                z = [1, 0]
            m["mem"] = mem[b]
            cp = np.concatenate([conv5.T, conv_b[:, None], a_b[z].T, x_b[z].T, lam[z].T], axis=1)
            m["chan_params"] = np.ascontiguousarray(cp.astype(np.float32))
            m["rg_a_w"] = np.ascontiguousarray(a_w[z])
            m["rg_x_w"] = np.ascontiguousarray(x_w[z])
            in_maps.append(m)
    return in_maps


def kernel(**inputs):
    in_maps = make_in_maps(inputs)
    nc = build_program()
    res = run_bass_kernel_spmd(nc, in_maps, core_ids=list(range(8)))
    out = np.empty((4, T, D), np.float32)
    ci = 0
    for b in range(4):
        for rev in range(2):
            o = np.asarray(res.results[ci]["out"])
            if rev == 0:
                out[b, 0:OWN] = o
            else:
                out[b, OWN:T] = o[::-1]
            ci += 1
    return out
```
